# Optimizing a Trainium2 kernel written in Bass

```python
import math
import jax
import jax.numpy as jnp
from jax import lax
import numpy as np

D_MODEL = 1024
BATCH = 8
SEQ = 2048
DEPTH = 2

CHUNK = 64
D_FF = 4096
LN_EPS = 1e-5
NORM_EPS = 1e-6
H_A = 4
DK_A = 128
DV_A = 128
CONV_K = 4
A_QK = H_A * DK_A
A_V = H_A * DV_A
H_B = 4
D_B = 128
B_PREV = 8
REL_CLIP = 128
B_W = H_B * D_B
H_C = 8
HKV_C = 2
G_C = H_C // HKV_C
D_C = 64
WINDOW = 128
C_PREV = WINDOW // CHUNK
C_Q = H_C * D_C
C_KV = HKV_C * D_C
N_BRANCH = 3
IN_SPLITS = (A_QK, A_QK, A_V, H_A, H_A, A_V, B_W, B_W, B_W, C_Q, C_KV, C_KV)
D_IN = sum(IN_SPLITS)

kernel_name = 'hybrid_chunk_streaming_block'


def layer_norm(x, g, b):
    xf = x.astype(jnp.float32)
    mu = jnp.mean(xf, -1, keepdims=True)
    xc = xf - mu
    var = jnp.mean(xc * xc, -1, keepdims=True)
    return (xc * lax.rsqrt(var + LN_EPS) * g.astype(jnp.float32) + b.astype(jnp.float32)).astype(x.dtype)


def swiglu(x, w_in, w_out):
    gate, up = jnp.split(x @ w_in, 2, axis=-1)
    return (jax.nn.silu(gate) * up) @ w_out


def causal_depthwise_conv(x, w):
    k = w.shape[0]
    return lax.conv_general_dilated(x, w[:, None, :], window_strides=(1,), padding=[(k - 1, 0)],
                                    dimension_numbers=('NWC', 'WIO', 'NWC'),
                                    feature_group_count=x.shape[-1])


def l2_normalize(x):
    return x * lax.rsqrt(jnp.sum(x * x, -1, keepdims=True) + NORM_EPS)


def to_chunks(a):
    b, t, h = a.shape[:3]
    a = a.reshape(b, t // CHUNK, CHUNK, h, *a.shape[3:])
    return jnp.moveaxis(a, (1, 3), (0, 2))


def chunk_gated_delta_rule(q, k, v, g, beta):
    b, t, h, dk = q.shape
    dv = v.shape[-1]
    q = to_chunks(q) * (dk ** -0.5)
    k = to_chunks(k)
    v = to_chunks(v)
    g = to_chunks(g)
    beta = to_chunks(beta)
    g_cum = jnp.cumsum(g, axis=-1)
    idx = jnp.arange(CHUNK)
    causal = idx[:, None] >= idx[None, :]
    strict = idx[:, None] > idx[None, :]
    decay = jnp.exp(jnp.where(causal, g_cum[..., :, None] - g_cum[..., None, :], -jnp.inf))
    k_beta = k * beta[..., None]
    lower = jnp.where(strict, jnp.einsum('nbhid,nbhjd->nbhij', k_beta, k) * decay, 0.0)
    eye = jnp.eye(CHUNK, dtype=jnp.float32)
    rhs = jnp.concatenate([v * beta[..., None], k_beta * jnp.exp(g_cum)[..., None]], axis=-1)
    uw = lax.linalg.triangular_solve(eye + lower, rhs, left_side=True, lower=True, unit_diagonal=True)
    u, w = uw[..., :dv], uw[..., dv:]
    attn = jnp.einsum('nbhid,nbhjd->nbhij', q, k) * decay
    q_dec = q * jnp.exp(g_cum)[..., None]
    g_last = g_cum[..., -1]
    k_dec = k * jnp.exp(g_last[..., None] - g_cum)[..., None]

    def step(state, inp):
        q_c, k_c, u_c, w_c, a_c, gl_c = inp
        v_new = u_c - jnp.einsum('bhcd,bhdv->bhcv', w_c, state)
        o_c = jnp.einsum('bhcd,bhdv->bhcv', q_c, state) + jnp.einsum('bhij,bhjv->bhiv', a_c, v_new)
        state = state * jnp.exp(gl_c)[..., None, None] + jnp.einsum('bhcd,bhcv->bhdv', k_c, v_new)
        return state, o_c

    s0 = jnp.zeros((b, h, dk, dv), jnp.float32)
    _, o = lax.scan(step, s0, (q_dec, k_dec, u, w, attn, g_last))
    o = jnp.moveaxis(o, (0, 2), (1, 3))
    return o.reshape(b, t, h, dv)


def gated_deltanet(qa, ka, va, beta_raw, a_raw, za, conv_w, a_log, dt_bias, norm_g):
    b, t, _ = qa.shape
    f32 = jnp.float32
    qkv = jax.nn.silu(causal_depthwise_conv(jnp.concatenate([qa, ka, va], -1), conv_w)).astype(f32)
    q, k, v = jnp.split(qkv, [A_QK, 2 * A_QK], axis=-1)
    q = l2_normalize(q.reshape(b, t, H_A, DK_A))
    k = l2_normalize(k.reshape(b, t, H_A, DK_A))
    v = v.reshape(b, t, H_A, DV_A)
    beta = jax.nn.sigmoid(beta_raw.astype(f32))
    g = -jnp.exp(a_log.astype(f32)) * jax.nn.softplus(a_raw.astype(f32) + dt_bias.astype(f32))
    o = chunk_gated_delta_rule(q, k, v, g, beta)
    z = za.reshape(b, t, H_A, DV_A).astype(f32)
    o = o * lax.rsqrt(jnp.mean(o * o, -1, keepdims=True) + NORM_EPS) * norm_g.astype(f32) * jax.nn.silu(z)
    return o.reshape(b, t, A_V).astype(qa.dtype)


def chunk_band(a, n_prev):
    b, t = a.shape[:2]
    n = t // CHUNK
    pad = [(0, 0), (n_prev * CHUNK, 0)] + [(0, 0)] * (a.ndim - 2)
    ap = jnp.pad(a, pad).reshape(b, n + n_prev, CHUNK, *a.shape[2:])
    band = jnp.stack([ap[:, j:j + n] for j in range(n_prev + 1)], axis=2)
    return band.reshape(b, n, (n_prev + 1) * CHUNK, *a.shape[2:])


def band_geometry(n, n_prev):
    i = jnp.arange(CHUNK)
    j = jnp.arange((n_prev + 1) * CHUNK)
    dist = n_prev * CHUNK + i[:, None] - j[None, :]
    valid = (jnp.arange(n)[:, None] - n_prev + j[None, :] // CHUNK) >= 0
    return dist, valid


def alibi_slopes(n):
    return 2.0 ** (-8.0 * jnp.arange(1, n + 1, dtype=jnp.float32) / n)


def chunk_relpos_attention(q, k, v, rel_bias):
    b, t, _ = q.shape
    n = t // CHUNK
    q = q.reshape(b, n, CHUNK, H_B, D_B)
    kb = chunk_band(k.reshape(b, t, H_B, D_B), B_PREV)
    vb = chunk_band(v.reshape(b, t, H_B, D_B), B_PREV)
    s = jnp.einsum('bnqhd,bnkhd->bnhqk', q, kb).astype(jnp.float32) * (D_B ** -0.5)
    dist, valid = band_geometry(n, B_PREV)
    rel_idx = jnp.clip(dist, -(CHUNK - 1), REL_CLIP) + (CHUNK - 1)
    bias = rel_bias.astype(jnp.float32)[:, rel_idx]
    s = jnp.where(valid[None, :, None, None, :], s + bias, -jnp.inf)
    p = jax.nn.softmax(s, axis=-1).astype(v.dtype)
    o = jnp.einsum('bnhqk,bnkhd->bnqhd', p, vb)
    return o.reshape(b, t, B_W)


def swa_sink_attention(q, k, v, sinks):
    b, t, _ = q.shape
    n = t // CHUNK
    q = q.reshape(b, n, CHUNK, HKV_C, G_C, D_C)
    kb = chunk_band(k.reshape(b, t, HKV_C, D_C), C_PREV)
    vb = chunk_band(v.reshape(b, t, HKV_C, D_C), C_PREV)
    s = jnp.einsum('bnqhgd,bnkhd->bnhgqk', q, kb).astype(jnp.float32) * (D_C ** -0.5)
    dist, valid = band_geometry(n, C_PREV)
    slopes = alibi_slopes(H_C).reshape(HKV_C, G_C)
    s = s - slopes[:, :, None, None] * jnp.abs(dist).astype(jnp.float32)
    s = jnp.where(valid[None, :, None, None, None, :], s, -jnp.inf)
    sink = jnp.broadcast_to(sinks.astype(jnp.float32).reshape(HKV_C, G_C)[:, :, None, None], s.shape[:-1] + (1,))
    p = jax.nn.softmax(jnp.concatenate([s, sink], axis=-1), axis=-1)[..., :-1].astype(v.dtype)
    o = jnp.einsum('bnhgqk,bnkhd->bnqhgd', p, vb)
    return o.reshape(b, t, C_Q)


def hybrid_mixer(x, w_in, b_in, conv_w, a_log, dt_bias, gdn_norm_g, rel_bias, sinks,
                 w_gate, b_gate, w_br_a, w_br_b, w_br_c, w_out):
    b, t, d = x.shape
    proj = x @ w_in + b_in
    split_at = np.cumsum(IN_SPLITS)[:-1].tolist()
    (qa, ka, va, beta_raw, a_raw, za, qb, kb, vb, qc, kc, vc) = jnp.split(proj, split_at, axis=-1)
    o_a = gated_deltanet(qa, ka, va, beta_raw, a_raw, za, conv_w, a_log, dt_bias, gdn_norm_g)
    o_b = chunk_relpos_attention(qb, kb, vb, rel_bias)
    o_c = swa_sink_attention(qc, kc, vc, sinks)
    gates = jax.nn.sigmoid(x @ w_gate + b_gate).reshape(b, t, N_BRANCH, d)
    merged = (gates[:, :, 0] * (o_a @ w_br_a) + gates[:, :, 1] * (o_b @ w_br_b)
              + gates[:, :, 2] * (o_c @ w_br_c))
    return merged @ w_out


def setup_inputs(seed: int = 0) -> dict:
    key = jax.random.key(seed)
    keys = list(jax.random.split(key, 32))
    f32 = jnp.float32
    L, D = DEPTH, D_MODEL
    sub_scale = (8.0 * DEPTH) ** -0.25

    def nrm(shape, scale):
        return jax.random.normal(keys.pop(), shape, f32) * scale

    def gain(shape):
        return 1.0 + nrm(shape, 0.02)

    x = nrm((BATCH, SEQ, D), 1.0)
    ln1_g = gain((L, D))
    ln1_b = nrm((L, D), 0.02)
    w_ff1_in = nrm((L, D, 2 * D_FF), D ** -0.5)
    w_ff1_out = nrm((L, D_FF, D), D_FF ** -0.5 * sub_scale)
    w_in = nrm((L, D, D_IN), D ** -0.5)
    b_in = nrm((L, D_IN), 0.02)
    conv_w = nrm((L, CONV_K, 2 * A_QK + A_V), CONV_K ** -0.5)
    a_log = jnp.log(jax.random.uniform(keys.pop(), (L, H_A), f32, 1.0, 16.0))
    dt = jnp.exp(jax.random.uniform(keys.pop(), (L, H_A), f32, math.log(1e-3), math.log(1e-1)))
    dt_bias = dt + jnp.log(-jnp.expm1(-dt))
    gdn_norm_g = gain((L, DV_A))
    rel_bias = nrm((L, H_B, CHUNK + REL_CLIP), 0.1)
    sinks = nrm((L, H_C), 0.5)
    w_gate = nrm((L, D, N_BRANCH * D), D ** -0.5)
    b_gate = nrm((L, N_BRANCH * D), 0.02)
    w_br_a = nrm((L, A_V, D), A_V ** -0.5 * sub_scale)
    w_br_b = nrm((L, B_W, D), B_W ** -0.5 * sub_scale)
    w_br_c = nrm((L, C_Q, D), C_Q ** -0.5 * sub_scale)
    w_out = nrm((L, D, D), D ** -0.5 * sub_scale)
    ln2_g = gain((L, D))
    ln2_b = nrm((L, D), 0.02)
    w_ff2_in = nrm((L, D, 2 * D_FF), D ** -0.5)
    w_ff2_out = nrm((L, D_FF, D), D_FF ** -0.5 * sub_scale)
    ln3_g = gain((L, D))
    ln3_b = nrm((L, D), 0.02)
    return {'x': x, 'ln1_g': ln1_g, 'ln1_b': ln1_b, 'w_ff1_in': w_ff1_in, 'w_ff1_out': w_ff1_out,
            'w_in': w_in, 'b_in': b_in, 'conv_w': conv_w, 'a_log': a_log, 'dt_bias': dt_bias,
            'gdn_norm_g': gdn_norm_g, 'rel_bias': rel_bias, 'sinks': sinks, 'w_gate': w_gate,
            'b_gate': b_gate, 'w_br_a': w_br_a, 'w_br_b': w_br_b, 'w_br_c': w_br_c, 'w_out': w_out,
            'ln2_g': ln2_g, 'ln2_b': ln2_b, 'w_ff2_in': w_ff2_in, 'w_ff2_out': w_ff2_out,
            'ln3_g': ln3_g, 'ln3_b': ln3_b}


def reference(x, ln1_g, ln1_b, w_ff1_in, w_ff1_out, w_in, b_in, conv_w, a_log, dt_bias,
              gdn_norm_g, rel_bias, sinks, w_gate, b_gate, w_br_a, w_br_b, w_br_c, w_out,
              ln2_g, ln2_b, w_ff2_in, w_ff2_out, ln3_g, ln3_b):
    alpha = (2.0 * DEPTH) ** 0.25
    for l in range(DEPTH):
        x = layer_norm(alpha * x + 0.5 * swiglu(x, w_ff1_in[l], w_ff1_out[l]), ln1_g[l], ln1_b[l])
        mix = hybrid_mixer(x, w_in[l], b_in[l], conv_w[l], a_log[l], dt_bias[l], gdn_norm_g[l],
                           rel_bias[l], sinks[l], w_gate[l], b_gate[l], w_br_a[l], w_br_b[l],
                           w_br_c[l], w_out[l])
        x = layer_norm(alpha * x + mix, ln2_g[l], ln2_b[l])
        x = layer_norm(alpha * x + 0.5 * swiglu(x, w_ff2_in[l], w_ff2_out[l]), ln3_g[l], ln3_b[l])
    return x
```

```python
import math
from contextlib import ExitStack

import numpy as np
import concourse.bass as bass
import concourse.mybir as mybir
from concourse.bass_utils import run_bass_kernel_spmd

F32 = mybir.dt.float32
BF16 = mybir.dt.bfloat16
AF = mybir.ActivationFunctionType
ALU = mybir.AluOpType

D = 1024
T = 2048
DEPTH = 2
F = 4096
KD = D // 128
ALPHA = (2.0 * DEPTH) ** 0.25
LN_EPS = 1e-5
NORM_EPS = 1e-6
D_IN = 4360

COMPUTE = ("pe", "act", "dve", "pool")
ENGS = ("pe", "act", "dve", "pool", "sp")
QUEUES = ("sp", "pool", "act")
NSEM = 12
SB_BASE = 16512
EPOCH = 20000


class Op:
    __slots__ = ("eng", "fn", "dma", "waits", "need_inc", "tick", "slot", "sidx")

    def __init__(self, eng, fn, dma):
        self.eng = eng
        self.fn = fn
        self.dma = dma
        self.waits = []
        self.need_inc = False
        self.tick = 0
        self.slot = 0
        self.sidx = 0


class Sched:
    def __init__(self, nc):
        self.nc = nc
        self.ops = []
        self.hist = {}
        self.tinfo = {}
        self.dma_ops = {q: [] for q in QUEUES}
        self.waited = {e: {} for e in ENGS}
        self.dma_waited = {e: set() for e in ENGS}
        self.npsum = 0

    def sb(self, name, shape, dtype, off):
        es = 4 if dtype == F32 else 2
        row = 1
        for s in shape[1:]:
            row *= s
        assert off % 32 == 0, (name, off)
        assert off + row * es <= 212800, (name, off, row * es)
        t = self.nc.alloc_sbuf_tensor_at(name, list(shape), dtype, offset=SB_BASE + off)
        self.tinfo[t.name] = ("sb", off, row, es)
        return t

    def ps(self, name, shape, dtype=F32):
        row = 1
        for s in shape[1:]:
            row *= s
        es = 4 if dtype == F32 else 2
        t = self.nc.alloc_psum_tensor(name, list(shape), dtype)
        self.tinfo[t.name] = ("ps_" + name, 0, row, es)
        return t

    def region(self, ap):
        name = ap.tensor.name
        info = self.tinfo.get(name)
        if info is None:
            return None
        space, base, row, es = info
        if space.startswith("ps_"):
            return (space, 0, 128, 0, 1 << 30)
        p_lo, f_lo = divmod(ap.offset, row)
        dims = ap.ap
        pstep, pcnt = dims[0]
        p_hi = p_lo + (pcnt - 1) * (pstep // row) + 1 if pstep else p_lo + 1
        span = 0
        for st, c in dims[1:]:
            span += (c - 1) * abs(st)
        return (space, p_lo, p_hi, base + f_lo * es, base + (f_lo + span + 1) * es)

    def add(self, eng, fn, reads=(), writes=(), dma=False):
        idx = len(self.ops)
        op = Op(eng, fn, dma)
        deps = {}
        rregs = [r for r in (self.region(a) for a in reads) if r is not None]
        wregs = [r for r in (self.region(a) for a in writes) if r is not None]
        for reg in rregs:
            for rec in self.hist.get(reg[0], ()):
                if rec[5] and rec[1] < reg[2] and reg[1] < rec[2] and rec[3] < reg[4] and reg[3] < rec[4]:
                    deps[rec[0]] = "raw"
        for reg in wregs:
            for rec in self.hist.get(reg[0], ()):
                if rec[1] < reg[2] and reg[1] < rec[2] and rec[3] < reg[4] and reg[3] < rec[4]:
                    if rec[0] not in deps:
                        deps[rec[0]] = "waw" if rec[5] else "war"
        for reg in wregs:
            lst = self.hist.setdefault(reg[0], [])
            lst[:] = [r for r in lst if not (reg[1] <= r[1] and r[2] <= reg[2] and reg[3] <= r[3] and r[4] <= reg[4])]
            lst.append((idx, reg[1], reg[2], reg[3], reg[4], True, eng, dma))
        for reg in rregs:
            lst = self.hist.setdefault(reg[0], [])
            if not dma:
                lst[:] = [r for r in lst if not ((not r[5]) and r[6] == eng and (not r[7]) and r[1:5] == reg[1:5])]
            lst.append((idx, reg[1], reg[2], reg[3], reg[4], False, eng, dma))
        best = {}
        for d, kind in deps.items():
            p = self.ops[d]
            if p.dma:
                if d not in self.dma_waited[eng]:
                    self.dma_waited[eng].add(d)
                    op.waits.append(d)
                continue
            if not dma and p.eng == eng and eng == "pe":
                continue
            if d > best.get(p.eng, -1):
                best[p.eng] = d
        for pe_, d in best.items():
            if d > self.waited[eng].get(pe_, -1):
                self.waited[eng][pe_] = d
                op.waits.append(d)
                self.ops[d].need_inc = True
        if dma:
            lst = self.dma_ops[eng]
            k = len(lst)
            op.slot = k % NSEM
            op.tick = 16 * (k // NSEM + 1)
            if k >= NSEM:
                prev = lst[k - NSEM]
                if prev not in self.dma_waited[eng]:
                    self.dma_waited[eng].add(prev)
                    op.waits.append(prev)
            lst.append(idx)
        self.ops.append(op)
        return idx

    def emit(self):
        nc = self.nc
        cnt = {e: 0 for e in COMPUTE}
        for op in self.ops:
            if not op.dma and op.need_inc:
                c = cnt[op.eng]
                op.sidx = c // EPOCH
                op.tick = c % EPOCH + 1
                cnt[op.eng] = c + 1
        with ExitStack() as st:
            sems = {e: [st.enter_context(nc.semaphore(f"s_{e}{i}")) for i in range(cnt[e] // EPOCH + 1)]
                    for e in COMPUTE}
            dsems = {q: [st.enter_context(nc.semaphore(f"d_{q}{i}")) for i in range(NSEM)]
                     for q in QUEUES if self.dma_ops[q]}
            block = st.enter_context(nc.Block())
            ops = self.ops

            def run(engname):
                def f(e):
                    for op in ops:
                        if op.eng != engname:
                            continue
                        for d in op.waits:
                            p = ops[d]
                            if p.dma:
                                e.wait_ge(dsems[p.eng][p.slot], p.tick)
                            else:
                                e.wait_ge(sems[p.eng][p.sidx], p.tick)
                        ins = op.fn(e)
                        if op.dma:
                            ins.then_inc(dsems[engname][op.slot], 16)
                        elif op.need_inc:
                            ins.then_inc(sems[engname][op.sidx], 1)
                    if engname in dsems:
                        lst = self.dma_ops[engname]
                        for d in lst[-NSEM:]:
                            p = ops[d]
                            e.wait_ge(dsems[engname][p.slot], p.tick)
                return f

            block.tensor(run("pe"))
            block.scalar(run("act"))
            block.vector(run("dve"))
            block.gpsimd(run("pool"))
            block.sync(run("sp"))


class B:
    def __init__(self, s):
        self.s = s

    def mm(self, out, lhsT, rhs, start, stop, **kw):
        self.s.add("pe", lambda e: e.matmul(out, lhsT, rhs, start=start, stop=stop, **kw),
                   reads=[lhsT, rhs], writes=[out])

    def tr(self, out, in_, ident):
        self.s.add("pe", lambda e: e.transpose(out, in_, ident), reads=[in_, ident], writes=[out])

    def act(self, out, in_, func, bias=None, scale=None, eng="act"):
        reads = [in_]
        kw = {}
        if bias is not None:
            kw["bias"] = bias
            if not isinstance(bias, (int, float)):
                reads.append(bias)
        if scale is not None:
            kw["scale"] = scale
            if not isinstance(scale, (int, float)):
                reads.append(scale)
        self.s.add("act", lambda e: e.activation(out, in_, func, **kw), reads=reads, writes=[out])

    def tt(self, out, in0, in1, op, eng="dve"):
        self.s.add(eng, lambda e: e.tensor_tensor(out, in0, in1, op), reads=[in0, in1], writes=[out])

    def ts(self, out, in0, s1, s2, op0, op1=None, eng="dve"):
        reads = [in0]
        for sc in (s1, s2):
            if sc is not None and not isinstance(sc, (int, float)):
                reads.append(sc)
        if op1 is None:
            self.s.add(eng, lambda e: e.tensor_scalar(out, in0, s1, None, op0), reads=reads, writes=[out])
        else:
            self.s.add(eng, lambda e: e.tensor_scalar(out, in0, s1, s2, op0, op1), reads=reads, writes=[out])

    def stt(self, out, in0, scalar, in1, op0, op1, eng="dve"):
        reads = [in0, in1]
        if not isinstance(scalar, (int, float)):
            reads.append(scalar)
        self.s.add(eng, lambda e: e.scalar_tensor_tensor(out, in0, scalar, in1, op0, op1),
                   reads=reads, writes=[out])

    def copy(self, out, in_, eng="dve"):
        self.s.add(eng, lambda e: e.tensor_copy(out, in_), reads=[in_], writes=[out])

    def memset(self, out, val, eng="dve"):
        self.s.add(eng, lambda e: e.memset(out, val), reads=[], writes=[out])

    def recip(self, out, in_):
        self.s.add("dve", lambda e: e.reciprocal(out, in_), reads=[in_], writes=[out])

    def dma(self, out, in_, q="sp"):
        self.s.add(q, lambda e: e.dma_start(out, in_), reads=[in_], writes=[out], dma=True)


OFF_X32 = 0
OFF_XB = 65536
OFF_CONST = 98304
OFF_ARENA = 106496
ARENA_END = 212800


class Ctx:
    pass


def setup_consts(c):
    s, b = c.s, c.b
    o = OFF_CONST
    c.ident = s.sb("ident", [128, 128], F32, o); o += 512
    c.onesb = s.sb("onesb", [128, 128], BF16, o); o += 256
    c.ones1 = s.sb("ones1", [128, 128], BF16, o); o += 256
    c.lnp = s.sb("lnp", [128, DEPTH * 3 * 2, KD], F32, o); o += DEPTH * 6 * KD * 4
    c.eps = s.sb("eps", [128, 1], F32, o); o += 32
    assert o <= OFF_ARENA
    c.const_end = o
    b.dma(c.ident[:], c.W['ident'], q='sp')
    b.memset(c.onesb[:], 1.0 / 1024.0)
    b.memset(c.ones1[:], 1.0)
    b.memset(c.eps[:], LN_EPS)


def load_ln_params(c, W):
    c.b.dma(c.lnp[:], W["lnp"], q="sp")


def load_x(c, x_dram):
    s, b = c.s, c.b
    stage = [s.sb(f"xstage{i}", [128, D], F32, OFF_ARENA + i * 4096) for i in range(2)]
    for tt in range(getattr(c, 'nload', T // 128)):
        stg = stage[tt % 2]
        b.dma(stg[:], x_dram[tt * 128:(tt + 1) * 128, :], q="sp")
        for half in range(2):
            pt = c.psum[(tt * 2 + half) % 2]
            for j in range(4):
                k = half * 4 + j
                b.tr(pt[:, j * 128:(j + 1) * 128], stg[:, k * 128:(k + 1) * 128], c.ident[:])
            dst32 = c.X32[:, half * 4:(half + 1) * 4, tt * 128:(tt + 1) * 128]
            dstb = c.XB[:, half * 4:(half + 1) * 4, tt * 128:(tt + 1) * 128]
            src = pt[:, :].rearrange("p (j t) -> p j t", j=4)
            b.copy(dst32, src, eng="dve")
            b.act(dstb, dst32, AF.Copy)


def store_x(c, out_dram, tiles=range(16)):
    s, b = c.s, c.b
    stage = [s.sb(f"ostage{i}_{c.uid()}", [128, D], F32, OFF_ARENA + i * 4096) for i in range(2)]
    for tt in tiles:
        stg = stage[tt % 2]
        for half in range(2):
            pt = c.psum[(tt * 2 + half) % 2]
            for j in range(4):
                k = half * 4 + j
                b.tr(pt[:, j * 128:(j + 1) * 128], c.X32[:, k, tt * 128:(tt + 1) * 128], c.ident[:])
            if half == 0:
                b.copy(stg[:, 0:512], pt[:, :], eng="dve")
            else:
                b.act(stg[:, 512:1024], pt[:, :], AF.Copy)
        b.dma(out_dram[tt * 128:(tt + 1) * 128, :], stg[:], q="sp")


def layer_norm_gen(c, cols, lnidx, scratch_off):
    s, b = c.s, c.b
    n = cols.stop - cols.start
    o = scratch_off
    rb = s.sb(f"ln_rb{c.uid()}", [128, KD, n], BF16, o); o += KD * n * 2
    rsq = s.sb(f"ln_rsq{c.uid()}", [128, KD, n], BF16, o); o += KD * n * 2
    mean = s.sb(f"ln_mean{c.uid()}", [128, n], F32, o); o += n * 4
    rstd = s.sb(f"ln_rstd{c.uid()}", [128, n], F32, o); o += n * 4
    s1 = c.psum[6]
    s2 = c.psum[7]
    for k in range(KD):
        b.act(rb[:, k, :], c.X32[:, k, cols], AF.Copy)
        b.act(rsq[:, k, :], c.X32[:, k, cols], AF.Square)
        yield
    for k in range(KD):
        b.mm(s1[:, 0:n], c.onesb[:], rb[:, k, :], start=(k == 0), stop=(k == KD - 1))
    yield
    for k in range(KD):
        b.mm(s2[:, 0:n], c.onesb[:], rsq[:, k, :], start=(k == 0), stop=(k == KD - 1))
    yield
    b.copy(mean[:], s1[:, 0:n])
    b.tt(rstd[:], mean[:], mean[:], ALU.mult)
    yield
    b.tt(rstd[:], s2[:, 0:n], rstd[:], ALU.subtract)
    b.ts(rstd[:], rstd[:], LN_EPS, None, ALU.add)
    yield
    b.recip(rstd[:], rstd[:])
    yield
    b.act(rstd[:], rstd[:], AF.Sqrt)
    yield
    gi = lnidx * 2
    xv = c.X32[:, :, cols]
    b.tt(xv, xv, mean[:].unsqueeze(1).to_broadcast([128, KD, n]), ALU.subtract)
    yield
    b.tt(xv, xv, rstd[:].unsqueeze(1).to_broadcast([128, KD, n]), ALU.mult)
    yield
    for k in range(KD):
        xs = c.X32[:, k, cols]
        b.act(c.XB[:, k, cols], xs, AF.Identity, bias=c.lnp[:, gi + 1, k:k + 1], scale=c.lnp[:, gi, k:k + 1])
        b.act(xs, xs, AF.Identity, bias=c.lnp[:, gi + 1, k:k + 1], scale=c.lnp[:, gi, k:k + 1])
        yield


def layer_norm_tile(c, cols, lnidx, scratch_off):
    for _ in layer_norm_gen(c, cols, lnidx, scratch_off):
        pass


class Pending:
    def __init__(self):
        self.gens = []

    def add(self, g):
        self.gens.append(g)

    def step(self, n=1):
        for _ in range(n):
            while self.gens:
                try:
                    next(self.gens[0])
                    break
                except StopIteration:
                    self.gens.pop(0)

    def drain(self):
        while self.gens:
            self.step()


def ffn_phase(c, w_in, w_out, lnidx, defer_last=False):
    s, b = c.s, c.b
    o = OFF_ARENA
    NTOK = 1024
    g = s.sb(f"ffn_g{c.uid()}", [128, 16, NTOK], BF16, o); o += 16 * NTOK * 2
    wg = []
    wu = []
    for i in range(3):
        wg.append(s.sb(f"ffn_wg{i}_{c.uid()}", [128, KD, 256], BF16, o)); o += KD * 256 * 2
        wu.append(s.sb(f"ffn_wu{i}_{c.uid()}", [128, KD, 256], BF16, o)); o += KD * 256 * 2
    wo = []
    for i in range(2):
        wo.append(s.sb(f"ffn_wo{i}_{c.uid()}", [128, 16, 128], BF16, o)); o += 16 * 128 * 2
    sg = []
    for i in range(2):
        sg.append(s.sb(f"ffn_sg{i}_{c.uid()}", [128, 512], F32, o)); o += 2048
    ln_off = o
    assert ln_off + 2 * KD * 512 * 2 + 4096 <= ARENA_END, ln_off
    w_in_v = w_in.rearrange("(k p) c -> p k c", p=128)
    w_out_v = w_out.rearrange("(f p) c -> p f c", p=128)
    wcnt = 0
    ocnt = 0
    pcnt = 0
    pending = getattr(c, "deferred", None) or Pending()
    c.deferred = None
    for st in range(T // NTOK):
        c0 = st * NTOK
        for hh in range(2):
            for fp in range(8):
                fc0 = hh * 16 + fp * 2
                wgb = wg[wcnt % 3]
                wub = wu[wcnt % 3]
                wcnt += 1
                b.dma(wgb[:], w_in_v[:, :, fc0 * 128:fc0 * 128 + 256], q="pool")
                b.dma(wub[:], w_in_v[:, :, F + fc0 * 128:F + fc0 * 128 + 256], q="pool")
                for j in range(2):
                    fl = fp * 2 + j
                    for nt in range(NTOK // 512):
                        cols = slice(c0 + nt * 512, c0 + (nt + 1) * 512)
                        pg = c.psum[0 + pcnt % 2]
                        pu = c.psum[2 + pcnt % 2]
                        sgb = sg[pcnt % 2]
                        pcnt += 1
                        for k in range(KD):
                            b.mm(pg[:], wgb[:, k, j * 128:(j + 1) * 128], c.XB[:, k, cols],
                                 start=(k == 0), stop=(k == KD - 1))
                        for k in range(KD):
                            b.mm(pu[:], wub[:, k, j * 128:(j + 1) * 128], c.XB[:, k, cols],
                                 start=(k == 0), stop=(k == KD - 1))
                        b.act(sgb[:], pg[:], AF.Silu)
                        b.stt(g[:, fl, nt * 512:(nt + 1) * 512], pu[:], 0.5, sgb[:], ALU.mult, ALU.mult)
                        pending.step(2)
            for dc in range(KD):
                wob = wo[ocnt % 2]
                ocnt += 1
                b.dma(wob[:], w_out_v[:, hh * 16:(hh + 1) * 16, dc * 128:(dc + 1) * 128], q="pool")
                for nt in range(NTOK // 512):
                    cols = slice(c0 + nt * 512, c0 + (nt + 1) * 512)
                    py = c.psum[4 + pcnt % 2]
                    pcnt += 1
                    for f in range(16):
                        b.mm(py[:], wob[:, f, :], g[:, f, nt * 512:(nt + 1) * 512],
                             start=(f == 0), stop=(f == 15))
                    xs = c.X32[:, dc, cols]
                    if hh == 0:
                        b.stt(xs, xs, ALPHA, py[:], ALU.mult, ALU.add)
                    else:
                        b.tt(xs, xs, py[:], ALU.add)
                    pending.step(1)
        pending.drain()
        for nt in range(NTOK // 512):
            cols = slice(c0 + nt * 512, c0 + (nt + 1) * 512)
            pending.add(layer_norm_gen(c, cols, lnidx, ln_off))
    if defer_last:
        c.deferred = pending
    else:
        pending.drain()


def flush_deferred(c):
    p = getattr(c, "deferred", None)
    if p is not None:
        p.drain()
    c.deferred = None


def build_program(phases=("ffn1", "mix", "ffn2"), layers=(0, 1)):
    nc = bass.Bass("TRN2", target_bir_lowering=False)
    W = {}

    def inp(name, shape):
        W[name] = nc.dram_tensor(name, list(shape), F32, kind="ExternalInput").ap()

    inp("x", [T, D])
    inp("ident", [128, 128])
    inp("lnp", [128, DEPTH * 6, KD])
    if "ffn1" in phases:
        inp("w_ff1_in", [DEPTH, D, 2 * F])
        inp("w_ff1_out", [DEPTH, F, D])
    if "ffn2" in phases:
        inp("w_ff2_in", [DEPTH, D, 2 * F])
        inp("w_ff2_out", [DEPTH, F, D])
    if "mix" in phases:
        inp("w_in", [DEPTH, D, D_IN])
        inp("b_in", [DEPTH, D_IN])
        inp("w_gate", [DEPTH, D, 3 * D])
        for nm in ("w_br_a", "w_br_b", "w_br_c"):
            inp(nm, [DEPTH, 512, D])
        inp("w_out", [DEPTH, D, D])
        inp("a_log", [DEPTH, 4])
        inp("dt_bias", [DEPTH, 4])
        inp("sinks", [DEPTH, 8])
        inp("MU", [128, 128]); inp("MUd", [128, 128]); inp("SC", [128, 128])
        inp("bfm", [DEPTH, 128, NFM]); inp("bgate_fm", [DEPTH, 128, 24]); inp("convw_fm", [DEPTH, 128, 12, 4])
        inp("ng_fm", [DEPTH, 128, 1])
        inp("BMg", [DEPTH, 128, 4, 640]); inp("MaskB", [128, 640]); inp("AMc", [128, 8, 2, 128])
    out = nc.dram_tensor("out", [T, D], F32, kind="ExternalOutput").ap()

    s = Sched(nc)
    c = Ctx()
    c.s = s
    c.b = B(s)
    c.nc = nc
    c.W = W
    c._uid = 0

    def uid():
        c._uid += 1
        return c._uid
    c.uid = uid
    c.X32 = s.sb("X32", [128, KD, T], F32, OFF_X32)
    c.XB = s.sb("XB", [128, KD, T], BF16, OFF_XB)
    c.psum = [s.ps(f"ps{i}", [128, 512], F32) for i in range(8)]
    setup_consts(c)
    load_ln_params(c, W)
    if "mix" in phases:
        setup_mixer_consts(c)
    load_x(c, W["x"])
    for l in layers:
        if "ffn1" in phases:
            ffn_phase(c, W["w_ff1_in"][l], W["w_ff1_out"][l], l * 3 + 0)
        if "mix" in phases:
            flush_deferred(c)
            mixer_phase(c, l, defer_last=("ffn2" in phases))
        if "ffn2" in phases:
            ffn_phase(c, W["w_ff2_in"][l], W["w_ff2_out"][l], l * 3 + 2, defer_last=True)
            if not (l + 1 < DEPTH and l + 1 in layers and "ffn1" in phases) and l != layers[-1]:
                flush_deferred(c)
    store_x(c, out, tiles=range(0, 8))
    flush_deferred(c)
    store_x(c, out, tiles=range(8, 16))
    s.emit()
    return nc, list(W.keys())


def host_consts(inputs):
    h = {}
    h["ident"] = np.eye(128, dtype=np.float32)
    names = [("ln1_g", "ln1_b"), ("ln2_g", "ln2_b"), ("ln3_g", "ln3_b")]
    lnp = np.zeros((128, DEPTH * 6, KD), np.float32)
    for l in range(DEPTH):
        for j, pair in enumerate(names):
            for q, nm in enumerate(pair):
                lnp[:, (l * 3 + j) * 2 + q, :] = np.asarray(inputs[nm][l], np.float32).reshape(KD, 128).T
    h["lnp"] = lnp
    idx = np.arange(128)
    same = (idx[:, None] // 64) == (idx[None, :] // 64)
    h["MU"] = ((idx[None, :] > idx[:, None]) & same).astype(np.float32)
    h["MUd"] = ((idx[None, :] >= idx[:, None]) & same).astype(np.float32)
    h["SC"] = same.astype(np.float32)
    if "b_in" not in inputs:
        return h
    b_in = np.asarray(inputs["b_in"], np.float32)
    bfm = np.zeros((DEPTH, 128, NFM), np.float32)
    for i, (nm, c0) in enumerate(FM_CHUNKS):
        bfm[:, :, i] = b_in[:, c0:c0 + 128]
    for kv in range(2):
        seg = b_in[:, 4104 + kv * 64:4104 + (kv + 1) * 64]
        bfm[:, :, len(FM_CHUNKS) + kv] = np.concatenate([seg, seg], axis=1)
    h["bfm"] = bfm
    h["bgate_fm"] = np.ascontiguousarray(
        np.asarray(inputs["b_gate"], np.float32).reshape(DEPTH, 24, 128).transpose(0, 2, 1))
    h["convw_fm"] = np.ascontiguousarray(
        np.asarray(inputs["conv_w"], np.float32).reshape(DEPTH, 4, 12, 128).transpose(0, 3, 2, 1))
    h["ng_fm"] = np.asarray(inputs["gdn_norm_g"], np.float32).reshape(DEPTH, 128, 1).copy()
    r = np.arange(128)[:, None, None]
    a = np.arange(5)[None, :, None]
    sq = np.arange(128)[None, None, :]
    dist = 128 * (4 - a) + sq - r
    ridx = np.clip(dist, -63, 128) + 63
    rb = np.asarray(inputs["rel_bias"], np.float32)
    h["BMg"] = np.ascontiguousarray(rb[:, :, ridx].transpose(0, 2, 1, 3, 4).reshape(DEPTH, 128, 4, 640))
    kk = 128 * (a - 4) + r
    dch = (sq // 64) - np.floor_divide(kk, 64)
    vis = (dch >= 0) & (dch <= 8)
    h["MaskB"] = np.where(vis, 0.0, -30000.0).astype(np.float32).reshape(128, 640)
    a2 = np.arange(2)[None, :, None]
    kk2 = 128 * (a2 - 1) + r
    dist2 = sq - kk2
    dch2 = (sq // 64) - np.floor_divide(kk2, 64)
    vis2 = (dch2 >= 0) & (dch2 <= 2)
    slopes = (2.0 ** (-8.0 * np.arange(1, 9, dtype=np.float32) / 8)).astype(np.float32)
    am = np.where(vis2[None], -slopes[:, None, None, None] * np.abs(dist2)[None].astype(np.float32), -30000.0)
    h["AMc"] = np.ascontiguousarray(am.transpose(1, 0, 2, 3)).astype(np.float32)
    return h


def make_in_maps(inputs, names, ncores=8):
    h = host_consts(inputs)
    shared = {}
    for n in names:
        if n == "x":
            continue
        shared[n] = h[n] if n in h else np.ascontiguousarray(inputs[n], dtype=np.float32)
    x = np.ascontiguousarray(inputs["x"], dtype=np.float32)
    in_maps = []
    for i in range(ncores):
        m = dict(shared)
        m["x"] = x[i]
        in_maps.append(m)
    return in_maps


def kernel(**inputs):
    nc, names = build_program()
    in_maps = make_in_maps(inputs, names)
    res = run_bass_kernel_spmd(nc, in_maps, core_ids=list(range(8)))
    return np.stack([r["out"] for r in res.results], axis=0)


FM_CHUNKS = ([("qa%d" % h, h * 128) for h in range(4)] + [("ka%d" % h, 512 + h * 128) for h in range(4)]
             + [("va%d" % h, 1024 + h * 128) for h in range(4)] + [("za%d" % h, 1544 + h * 128) for h in range(4)]
             + [("qb%d" % h, 2056 + h * 128) for h in range(4)] + [("kb%d" % h, 2568 + h * 128) for h in range(4)]
             + [("qc%d" % j, 3592 + j * 128) for j in range(4)])
FM_IDX = {nm: i for i, (nm, _) in enumerate(FM_CHUNKS)}
FM_COL = {nm: c0 for nm, c0 in FM_CHUNKS}
NFM = len(FM_CHUNKS) + 2
OFF_MCONST = OFF_CONST + 2048


def setup_mixer_consts(c):
    s, b = c.s, c.b
    o = OFF_MCONST
    c.ones32 = s.sb("ones32", [128, 128], F32, o); o += 512
    c.MU = s.sb("MU", [128, 128], F32, o); o += 512
    c.MUd = s.sb("MUd", [128, 128], F32, o); o += 512
    c.SC = s.sb("SC", [128, 128], F32, o); o += 512
    c.bfm = s.sb("bfm", [128, NFM], F32, o); o += 160
    c.bgate = s.sb("bgate", [128, 24], F32, o); o += 96
    c.convw = s.sb("convw", [128, 12, 4], F32, o); o += 192
    c.ng = s.sb("ng", [128, 1], F32, o); o += 32
    c.esink = s.sb("esink", [128, 8], F32, o); o += 32
    c.nA = s.sb("nA", [128, 4], F32, o); o += 32
    c.dtb = s.sb("dtb", [128, 4], F32, o); o += 32
    c.brow_ba = s.sb("brow_ba", [128, 8], F32, o); o += 32
    c.identb = s.sb("identb", [128, 128], BF16, o); o += 256
    assert o <= OFF_ARENA, o
    b.memset(c.ones32[:], 1.0)
    b.copy(c.identb[:], c.ident[:])
    b.dma(c.MU[:], c.W["MU"], q="sp")
    b.dma(c.MUd[:], c.W["MUd"], q="sp")
    b.dma(c.SC[:], c.W["SC"], q="sp")


def load_mixer_layer_consts(c, l):
    b, W = c.b, c.W
    b.dma(c.bfm[:], W["bfm"][l], q="sp")
    b.dma(c.bgate[:], W["bgate_fm"][l], q="sp")
    b.dma(c.convw[:], W["convw_fm"][l], q="sp")
    b.dma(c.ng[:], W["ng_fm"][l], q="sp")
    b.dma(c.esink[:], W["sinks"][l:l + 1, :].partition_broadcast(128), q="sp")
    b.act(c.esink[:], c.esink[:], AF.Exp)
    b.dma(c.nA[:], W["a_log"][l:l + 1, :].partition_broadcast(128), q="sp")
    b.act(c.nA[:], c.nA[:], AF.Exp)
    b.ts(c.nA[:], c.nA[:], -1.0, None, ALU.mult)
    b.dma(c.dtb[:], W["dt_bias"][l:l + 1, :].partition_broadcast(128), q="sp")
    b.dma(c.brow_ba[:], W["b_in"][l:l + 1, 1536:1544].partition_broadcast(128), q="sp")


class Alloc:
    def __init__(self, c, start):
        self.c = c
        self.o = start

    def __call__(self, name, shape, dtype):
        es = 4 if dtype == F32 else 2
        row = 1
        for x_ in shape[1:]:
            row *= x_
        nbytes = (row * es + 31) // 32 * 32
        t = self.c.s.sb(f"{name}_{self.c.uid()}", shape, dtype, self.o)
        self.o += nbytes
        assert self.o <= ARENA_END, (name, self.o)
        return t


def proj_fm(c, w_in_l, wbufs, cnt, col0, evac, dup64=False, nts=(0, 1, 2, 3)):
    b = c.b
    w_v = w_in_l.rearrange("(k p) c -> p k c", p=128)
    if len(cnt) < 2:
        cnt.append(0)
    wt = wbufs[cnt[1] % len(wbufs)]
    if dup64:
        b.dma(wt[:, :, 0:64], w_v[:, :, col0:col0 + 64], q="pool")
        b.dma(wt[:, :, 64:128], w_v[:, :, col0:col0 + 64], q="pool")
    else:
        b.dma(wt[:], w_v[:, :, col0:col0 + 128], q="pool")
    for nt in nts:
        ps = c.psum[cnt[0] % 2]
        cnt[0] += 1
        for k in range(KD):
            b.mm(ps[:], wt[:, k, :], c.XB[:, k, nt * 512:(nt + 1) * 512], start=(k == 0), stop=(k == KD - 1))
        evac(nt, ps)
    cnt[1] = cnt[1] + 1 if len(cnt) > 1 else 0


def phase_B(c, l, o_b, A):
    s, b, W = c.s, c.b, c.W
    w_in_l = W["w_in"][l]
    qb = A("qb", [128, 4, T], BF16)
    kb = A("kb", [128, 4, T], BF16)
    vb = A("vb", [128, 16, 512], BF16)
    wb = [A(f"wB{i}", [128, KD, 128], BF16) for i in range(2)]
    wv = A("wBv", [128, KD, 512], BF16)
    brow = A("browB", [128, 512], F32)
    BM = A("BM", [128, 4, 640], F32)
    MB = A("MB", [128, 640], F32)
    tt_ = [A(f"tB{i}", [128, 640], F32) for i in range(2)]
    pT = [A(f"pB{i}", [128, 640], BF16) for i in range(2)]
    rden = [A(f"rdB{i}", [128, 128], F32) for i in range(2)]
    cnt = [0]
    for h in range(4):
        for nm, dst in ((f"qb{h}", qb), (f"kb{h}", kb)):
            bi = FM_IDX[nm]
            proj_fm(c, w_in_l, wb, cnt, FM_COL[nm],
                    lambda nt, ps, dst=dst, h=h, bi=bi: b.act(dst[:, h, nt * 512:(nt + 1) * 512], ps[:], AF.Identity,
                                                              bias=c.bfm[:, bi:bi + 1]))
    w_v = w_in_l.rearrange("(k p) c -> p k c", p=128)
    b.dma(wv[:], w_v[:, :, 3080:3592], q="pool")
    b.dma(brow[:], W["b_in"][l:l + 1, 3080:3592].partition_broadcast(128), q="sp")
    for tile in range(16):
        ps = c.psum[tile % 2]
        for k in range(KD):
            b.mm(ps[:], c.XB[:, k, tile * 128:(tile + 1) * 128], wv[:, k, :], start=(k == 0), stop=(k == KD - 1))
        b.tt(vb[:, tile, :], ps[:], brow[:], ALU.add)
    b.dma(BM[:], W["BMg"][l], q="sp")
    b.dma(MB[:], W["MaskB"], q="sp")
    for h in range(4):
        b.tt(BM[:, h, :], BM[:, h, :], MB[:], ALU.add)
    scale = 128.0 ** -0.5

    def bufs(it):
        return (c.psum[2 + (it % 2) * 3], c.psum[3 + (it % 2) * 3], c.psum[4 + (it % 2) * 3],
                tt_[it % 2], pT[it % 2], rden[it % 2])

    def stage1(it, h, n):
        pa, pb_, po, t_, p_, rd = bufs(it)
        a0 = max(0, 4 - n)
        for a in range(a0, 5):
            m = n - 4 + a
            dst = pa[:, a * 128:(a + 1) * 128] if a < 4 else pb_[:, 0:128]
            b.mm(dst, kb[:, h, m * 128:(m + 1) * 128], qb[:, h, n * 128:(n + 1) * 128], start=True, stop=True)
        if a0 < 4:
            b.stt(t_[:, a0 * 128:512], pa[:, a0 * 128:512], scale, BM[:, h, a0 * 128:512], ALU.mult, ALU.add)
        b.stt(t_[:, 512:640], pb_[:, 0:128], scale, BM[:, h, 512:640], ALU.mult, ALU.add)
        b.act(p_[:, a0 * 128:640], t_[:, a0 * 128:640], AF.Exp)

    def stage2(it, h, n):
        pa, pb_, po, t_, p_, rd = bufs(it)
        a0 = max(0, 4 - n)
        for a in range(a0, 5):
            m = n - 4 + a
            b.mm(po[:, 0:128], vb[:, m, h * 128:(h + 1) * 128], p_[:, a * 128:(a + 1) * 128],
                 start=(a == a0), stop=(a == 4))
        for a in range(a0, 5):
            b.mm(po[:, 128:256], c.ones1[:], p_[:, a * 128:(a + 1) * 128], start=(a == a0), stop=(a == 4))
        b.recip(rd[:], po[:, 128:256])
        b.tt(o_b[:, h, n * 128:(n + 1) * 128], po[:, 0:128], rd[:], ALU.mult)

    iters = [(h, n) for h in range(4) for n in range(16)]
    stage1(0, *iters[0])
    for i, (h, n) in enumerate(iters):
        if i + 1 < len(iters):
            stage1(i + 1, *iters[i + 1])
        stage2(i, h, n)


def phase_C(c, l, o_c, A):
    s, b, W = c.s, c.b, c.W
    w_in_l = W["w_in"][l]
    w_v = w_in_l.rearrange("(k p) c -> p k c", p=128)
    qc = A("qc", [128, 2, T], BF16)
    kc = [A("kc0", [128, T], BF16), A("kc1", [128, T], BF16)]
    vc = A("vc", [128, 16, 128], BF16)
    wb = [A(f"wC{i}", [128, KD, 128], BF16) for i in range(2)]
    wv = A("wCv", [128, KD, 128], BF16)
    brow = A("browC", [128, 128], F32)
    AM = A("AM", [128, 8, 2, 128], F32)
    tt_ = [A(f"tC{i}", [128, 512], F32) for i in range(4)]
    pT = [A(f"pC{i}", [128, 512], BF16) for i in range(4)]
    dsb = [A(f"dC{i}", [128, 512], F32) for i in range(2)]
    cnt = [0]
    b.dma(AM[:], W["AMc"], q="sp")
    it = 0
    for kv in range(2):
        for jj in range(2):
            j = kv * 2 + jj
            bi = FM_IDX[f"qc{j}"]
            proj_fm(c, w_in_l, wb, cnt, FM_COL[f"qc{j}"],
                    lambda nt, ps, jj=jj, bi=bi: b.act(qc[:, jj, nt * 512:(nt + 1) * 512], ps[:], AF.Identity,
                                                       bias=c.bfm[:, bi:bi + 1]))
        bi = len(FM_CHUNKS) + kv

        def ev(nt, ps, bi=bi):
            cs_ = slice(nt * 512, (nt + 1) * 512)
            b.act(kc[0][:, cs_], ps[:], AF.Identity, bias=c.bfm[:, bi:bi + 1])
            b.copy(kc[1][:, cs_], kc[0][:, cs_])
            b.memset(kc[0][64:128, cs_], 0.0)
            b.memset(kc[1][0:64, cs_], 0.0)
        proj_fm(c, w_in_l, wb, cnt, 4104 + kv * 64, ev, dup64=True)
        for q2 in range(2):
            b.dma(wv[:, :, q2 * 64:(q2 + 1) * 64], w_v[:, :, 4232 + kv * 64:4232 + (kv + 1) * 64], q="pool")
            b.dma(brow[:, q2 * 64:(q2 + 1) * 64],
                  W["b_in"][l:l + 1, 4232 + kv * 64:4232 + (kv + 1) * 64].partition_broadcast(128), q="sp")
        for tile in range(16):
            ps = c.psum[tile % 2]
            for k in range(KD):
                b.mm(ps[:, 0:128], c.XB[:, k, tile * 128:(tile + 1) * 128], wv[:, k, :],
                     start=(k == 0), stop=(k == KD - 1))
            b.tt(vc[:, tile, :], ps[:, 0:128], brow[:], ALU.add)
        def cbufs(it):
            return ([c.psum[2 + (it % 2) * 3], c.psum[3 + (it % 2) * 3]], c.psum[4 + (it % 2) * 3], dsb[it % 2])

        def cstage1(it, n, kv=kv):
            pss, po, ds_ = cbufs(it)
            a0 = 1 if n == 0 else 0
            for a in range(a0, 2):
                m = n - 1 + a
                ps_a = pss[a]
                for g in range(4):
                    jj, half = g // 2, g % 2
                    b.mm(ps_a[:, g * 128:(g + 1) * 128], kc[half][:, m * 128:(m + 1) * 128],
                         qc[:, jj, n * 128:(n + 1) * 128], start=True, stop=True)
                t_ = tt_[(it % 2) * 2 + a]
                p_ = pT[(it % 2) * 2 + a]
                b.stt(t_[:].rearrange("p (g q) -> p g q", g=4), ps_a[:].rearrange("p (g q) -> p g q", g=4), 0.125,
                      AM[:, kv * 4:(kv + 1) * 4, a, :], ALU.mult, ALU.add)
                b.act(p_[:], t_[:], AF.Exp)

        def cstage2(it, n, kv=kv):
            pss, po, ds_ = cbufs(it)
            pd = pss[1]
            a0 = 1 if n == 0 else 0
            for a in range(a0, 2):
                m = n - 1 + a
                b.mm(po[:], vc[:, m, :], pT[(it % 2) * 2 + a][:], start=(a == a0), stop=(a == 1))
            for a in range(a0, 2):
                b.mm(pd[:], c.ones1[:], pT[(it % 2) * 2 + a][:], start=(a == a0), stop=(a == 1))
            b.tt(ds_[:].rearrange("p (g q) -> p g q", g=4), pd[:].rearrange("p (g q) -> p g q", g=4),
                 c.esink[:, kv * 4:(kv + 1) * 4].unsqueeze(2).to_broadcast([128, 4, 128]), ALU.add)
            b.recip(ds_[:], ds_[:])
            for g in range(4):
                hq = kv * 4 + g
                j, half = hq // 2, hq % 2
                pr = slice(half * 64, half * 64 + 64)
                b.tt(o_c[pr, j, n * 128:(n + 1) * 128], po[pr, g * 128:(g + 1) * 128], ds_[pr, g * 128:(g + 1) * 128],
                     ALU.mult)

        cstage1(0, 0)
        for n in range(16):
            if n + 1 < 16:
                cstage1(n + 1, n + 1)
            cstage2(n, n)


def phase_A(c, l, o_a, A):
    s, b, W = c.s, c.b, c.W
    w_in_l = W["w_in"][l]
    w_v = w_in_l.rearrange("(k p) c -> p k c", p=128)
    Q = 512
    wb = [A(f"wA{i}", [128, KD, 128], BF16) for i in range(2)]
    wba = A("wAba", [128, KD, 8], BF16)
    ba = A("ba", [128, 16, 8], F32)
    beta = A("beta", [128, 16, 4], F32)
    gg = A("gg", [128, 16, 4], F32)
    KA = A("KA", [128, 4, Q], F32)
    VA = A("VA", [128, 4, Q], F32)
    QAb = A("QAb", [128, 4, Q], BF16)
    KAb = A("KAb", [128, 4, Q], BF16)
    SZ = A("SZ", [128, 4, Q], BF16)
    S4 = A("S4", [128, 4, 128], F32)
    S4b = A("S4b", [128, 4, 128], BF16)
    carry = A("carry", [128, 12, 4], F32)
    sm = A("smA", [128, 16], F32)
    smS = A("smS", [128, 16], F32)
    rn_all = A("rn_all", [128, 8, 4], F32)
    slot_off = A.o
    Gm, GCrow, E, EMb, EMd, ktok, EGrow = [A(f"fslot{i}", [128, 4, 128], F32) for i in range(7)]
    Rw, Up, UpT = [A(f"bslot{i}", [128, 4, 128], BF16) for i in range(3)]
    HO = []
    for i in range(2):
        HO.append(dict(
            eglast=A(f"eglast{i}", [128, 4, 2], F32),
            vtok=A(f"vtok{i}", [128, 4, 128], BF16), kdec0=A(f"kdec0{i}", [128, 4, 128], BF16),
            kdec1=A(f"kdec1{i}", [128, 4, 128], BF16), Xf=A(f"Xf{i}", [128, 4, 128], BF16),
            nw0T=A(f"nw0T{i}", [128, 4, 128], BF16), attnT=A(f"attnT{i}", [128, 4, 128], BF16),
            qdec=A(f"qdec{i}", [128, 4, 128], BF16)))
    vnew = A("vnew", [128, 4, 128], BF16)
    otok = A("otok", [128, 4, 128], F32)
    junk = A("junk", [128, 4, 128], BF16)
    u = c.uid()
    rawps = [s.sb(f"rawpA{i}{u}", [128, 3 + Q], F32, slot_off + i * 2080) for i in range(2)]
    sqbs = [s.sb(f"sqbA{i}{u}", [128, Q], BF16, slot_off + 4160 + i * 1024) for i in range(2)]
    qtmps = [s.sb(f"qtmpA{i}{u}", [128, Q], F32, slot_off + 6208 + i * 2048) for i in range(2)]
    diags = [s.sb(f"diagA0{u}", [128, 4, 128], F32, slot_off + 10304),
             s.sb(f"diagA1{u}", [128, 4, 128], F32, slot_off)]
    W2 = [s.sb(f"wA2a{u}", [128, KD, 256], BF16, s.tinfo[wb[0].name][1]),
          s.sb(f"wA2b{u}", [128, KD, 256], BF16, slot_off + 12352)]
    wcn = [0]
    itc = [0]
    P = s.sb(f"gP{u}", [128, 4, 128], BF16, slot_off + 2048)
    PT = s.sb(f"gPT{u}", [128, 4, 128], BF16, slot_off + 2048 + 1024)
    X = s.sb(f"gX{u}", [128, 4, 128], BF16, slot_off + 4096)
    X2 = s.sb(f"gX2{u}", [128, 4, 128], BF16, slot_off + 4096 + 1024)

    def bc(ap4, np_=128):
        return ap4.unsqueeze(2).to_broadcast([np_, 4, 128])

    def hb(ap2):
        return ap2.unsqueeze(1).to_broadcast([128, 4, 128])

    def flat(t4):
        return t4[:].rearrange("p h n -> p (h n)")

    cnt = [0]
    for i in range(2):
        b.memset(HO[i]["kdec0"][:], 0.0)
        b.memset(HO[i]["kdec1"][:], 0.0)
    b.memset(S4[:], 0.0)
    b.memset(S4b[:], 0.0)
    b.memset(vnew[:], 0.0)
    b.dma(wba[:], w_v[:, :, 1536:1544], q="pool")
    for tile in range(16):
        ps = c.psum[tile % 2]
        for k in range(KD):
            b.mm(ps[:, 0:8], c.XB[:, k, tile * 128:(tile + 1) * 128], wba[:, k, :], start=(k == 0), stop=(k == KD - 1))
        b.tt(ba[:, tile, :], ps[:, 0:8], c.brow_ba[:], ALU.add)
    b.act(beta[:], ba[:, :, 0:4], AF.Sigmoid)
    b.tt(gg[:], ba[:, :, 4:8], c.dtb[:].unsqueeze(1).to_broadcast([128, 16, 4]), ALU.add)
    b.act(gg[:], gg[:], AF.Exp)
    b.act(gg[:], gg[:], AF.Ln, bias=1.0)
    b.tt(gg[:], gg[:], c.nA[:].unsqueeze(1).to_broadcast([128, 16, 4]), ALU.mult)

    def proj_q(wt, hh, qi, evac):
        ps = c.psum[cnt[0] % 2]
        cnt[0] += 1
        for k in range(KD):
            b.mm(ps[:], wt[:, k, hh * 128:(hh + 1) * 128], c.XB[:, k, qi * 512:(qi + 1) * 512],
                 start=(k == 0), stop=(k == KD - 1))
        evac(ps)

    def conv_a(it_, qi):
        nm = it_["nm"]
        ci = FM_COL[nm] // 128
        bi = FM_IDX[nm]
        if it_["kind"] == "z":
            h = it_["h"]
            proj_q(it_["wt"], it_["hh"], qi,
                   lambda ps: b.act(SZ[:, h, :], ps[:], AF.Silu, bias=c.bfm[:, bi:bi + 1]))
            return
        cr = carry[:, ci, 0:3]
        rawp = rawps[itc[0] % 2]
        itc[0] += 1
        it_["rawp"] = rawp
        if qi == 0:
            b.memset(rawp[:, 0:3], 0.0)
        else:
            b.copy(rawp[:, 0:3], cr)
        proj_q(it_["wt"], it_["hh"], qi,
               lambda ps: b.act(rawp[:, 3:3 + Q], ps[:], AF.Identity, bias=c.bfm[:, bi:bi + 1]))
        b.copy(cr, rawp[:, Q:Q + 3])

    def conv_b(it_, qi):
        if it_["kind"] == "z":
            return
        ci = FM_COL[it_["nm"]] // 128
        rawp, acc_, out_ = it_["rawp"], it_["acc"], it_["out"]
        b.ts(acc_, rawp[:, 0:Q], c.convw[:, ci, 0:1], None, ALU.mult)
        for j in range(1, 4):
            b.stt(acc_, rawp[:, j:j + Q], c.convw[:, ci, j:j + 1], acc_, ALU.mult, ALU.add)
        b.act(out_, acc_, AF.Silu)
        if it_["ss"] is not None:
            sumsq(out_, it_["ss"])

    def load_pair(col0):
        wt = W2[wcn[0] % 2]
        wcn[0] += 1
        b.dma(wt[:], w_v[:, :, col0:col0 + 256], q="pool")
        return wt

    def sumsq(x_, item):
        sqb = sqbs[item % 2]
        b.act(sqb[:], x_, AF.Square)
        for tl in range(4):
            b.mm(c.psum[2][:, item * 4 + tl:item * 4 + tl + 1], sqb[:, tl * 128:(tl + 1) * 128], c.ones1[:, 0:1],
                 start=True, stop=True)

    def norm_a(item, slot):
        dg = diags[slot % 2]
        pb_ = c.psum[3 + slot % 2]
        b.tt(dg[:], hb(c.ident[:]), rn_all[:, item, :].unsqueeze(2).to_broadcast([128, 4, 128]), ALU.mult)
        b.mm(pb_[:], c.ones32[:], dg[:].rearrange("p h n -> p (h n)"), start=True, stop=True)

    def norm_b(slot, x_, scale, xb_=None):
        pb_ = c.psum[3 + slot % 2]
        b.stt(x_, x_, scale, pb_[:], ALU.mult, ALU.mult)
        if xb_ is not None:
            b.copy(xb_, x_)

    F1, F2, F3, F4 = (c.psum[i] for i in (2, 3, 4, 5))
    S1, S2, S3, S4p = (c.psum[i] for i in (0, 1, 6, 7))
    F3b = F3.bitcast(BF16)

    def v4(p):
        return p[:].rearrange("p (h n) -> p h n", h=4)

    def hsl(h):
        return slice(h * 128, (h + 1) * 128)

    def front(j, jl, ho):
        cols = slice(jl * 128, (jl + 1) * 128)
        g4 = gg[:, j, :]
        beta4 = beta[:, j, :]
        b.tt(Gm[:], hb(c.MUd[:]), bc(g4), ALU.mult)
        b.mm(F1[:], c.ones32[:], flat(Gm), start=True, stop=True)
        b.mm(F2[:, 0:4], c.MUd[:], g4, start=True, stop=True)
        b.mm(F2[:, 4:8], c.SC[:], g4, start=True, stop=True)
        yield
        b.copy(flat(GCrow), F1[:])
        b.copy(sm[:, 0:8], F2[:, 0:8])
        for h in range(4):
            b.tr(F3[:, hsl(h)], KA[:, h, cols], c.ident[:])
        for h in range(4):
            b.tr(F4[:, hsl(h)], VA[:, h, cols], c.ident[:])
        yield
        b.tt(sm[:, 8:12], sm[:, 4:8], sm[:, 0:4], ALU.subtract)
        b.act(sm[:, 8:12], sm[:, 8:12], AF.Exp)
        b.act(sm[:, 12:16], sm[:, 0:4], AF.Exp)
        b.tt(E[:], GCrow[:], bc(sm[:, 0:4]), ALU.subtract)
        yield
        b.ts(E[:], E[:], 0.0, None, ALU.min)
        b.act(EGrow[:], GCrow[:], AF.Exp)
        b.copy(flat(ktok), F3[:])
        b.act(flat(ho["vtok"]), F4[:], AF.Copy)
        yield
        b.act(E[:], E[:], AF.Exp)
        for h in range(4):
            b.mm(F1[:, hsl(h)], KAb[:, h, cols], KAb[:, h, cols], start=True, stop=True)
        for h in range(4):
            b.mm(F2[:, hsl(h)], KAb[:, h, cols], QAb[:, h, cols], start=True, stop=True)
        b.copy(ho["eglast"][:, :, 0:1], EGrow[:, :, 63:64])
        b.copy(ho["eglast"][:, :, 1:2], EGrow[:, :, 127:128])
        yield
        b.tt(EMb[:], E[:], hb(c.MU[:]), ALU.mult)
        b.tt(EMd[:], E[:], hb(c.MUd[:]), ALU.mult)
        b.tt(Rw[:], ktok[:], bc(sm[:, 12:16]), ALU.mult)
        yield
        b.tt(EMb[:], EMb[:], bc(beta4), ALU.mult)
        b.tt(ho["kdec0"][0:64], ktok[0:64], bc(sm[0:64, 8:12], 64), ALU.mult)
        b.tt(ho["kdec1"][64:128], ktok[64:128], bc(sm[64:128, 8:12], 64), ALU.mult)
        yield
        b.tt(Up[:], v4(F1), EMb[:], ALU.mult)
        b.tt(ho["qdec"][:], QAb[:, :, cols], EGrow[:], ALU.mult)
        yield
        b.tt(ho["attnT"][:], v4(F2), EMd[:], ALU.mult)
        for h in range(4):
            b.tr(F3b[:, hsl(h)], Up[:, h, :], c.identb[:])
        b.tt(X[:], hb(c.ident[:]), Up[:], ALU.subtract)
        yield
        b.copy(flat(UpT), F3b[:, 0:512])
        yield
        cur, nxt = (Up, UpT), (P, PT)
        Xc, Xn = X, X2
        for lev in range(5):
            if lev == 4:
                Xn = ho["Xf"]
            if lev < 4:
                for h in range(4):
                    b.mm(F1[:, hsl(h)], cur[1][:, h, :], cur[0][:, h, :], start=True, stop=True)
            for h in range(4):
                b.mm(F2[:, hsl(h)], cur[0][:, h, :], cur[1][:, h, :], start=True, stop=True)
            yield
            b.act(flat(nxt[1]), F2[:], AF.Copy)
            if lev < 4:
                b.copy(flat(nxt[0]), F1[:])
            yield
            for h in range(4):
                b.mm(F4[:, hsl(h)], nxt[1][:, h, :], Xc[:, h, :], start=True, stop=True)
            yield
            b.tt(Xn[:], v4(F4), Xc[:], ALU.add)
            yield
            cur, nxt = nxt, cur
            Xc, Xn = Xn, Xc
        for h in range(4):
            b.mm(F3[:, hsl(h)], Rw[:, h, :], ho["Xf"][:, h, :], start=True, stop=True)
        yield
        b.act(flat(ho["nw0T"]), F3[:], AF.Copy, scale=-1.0)
        yield

    def state(j, jl, ho):
        cols = slice(jl * 128, (jl + 1) * 128)
        gcols = slice(j * 128, (j + 1) * 128)
        kdec = (ho["kdec0"], ho["kdec1"])
        for ch in range(2):
            cs = slice(ch * 64, ch * 64 + 64)
            pV, pO, pS = (S1, S2, S3) if ch == 0 else (S4p, S2, S3)
            for h in range(4):
                b.mm(pV[:, hsl(h)], ho["Xf"][:, h, :], ho["vtok"][:, h, :], start=True, stop=False)
                b.mm(pV[:, hsl(h)], ho["nw0T"][:, h, :], S4b[:, h, :], start=False, stop=True)
            yield
            b.tt(vnew[cs], v4(pV)[cs], bc(beta[cs, j, :], 64), ALU.mult)
            yield
            for h in range(4):
                b.mm(pS[:, hsl(h)], kdec[ch][:, h, :], vnew[:, h, :], start=True, stop=True)
            for h in range(4):
                b.mm(pO[:, hsl(h)], ho["qdec"][:, h, :], S4b[:, h, :], start=True, stop=False)
                b.mm(pO[:, hsl(h)], ho["attnT"][:, h, :], vnew[:, h, :], start=False, stop=True)
            b.tt(S4[:], S4[:], ho["eglast"][:, :, ch:ch + 1].to_broadcast([128, 4, 128]), ALU.mult)
            yield
            b.tt(S4[:], S4[:], v4(pS), ALU.add)
            b.act(otok[cs], v4(pO)[cs], AF.Copy)
            yield
            b.act(S4b[:], S4[:], AF.Copy)
            yield
        b.tt(junk[:], otok[:], otok[:], ALU.mult)
        yield
        c.s.add("dve", lambda e, o_=smS[:, 0:4], i_=junk[:]: e.reduce_sum(o_, i_, mybir.AxisListType.X),
                reads=[junk[:]], writes=[smS[:, 0:4]])
        yield
        b.ts(smS[:, 4:8], smS[:, 0:4], 1.0 / 128.0, NORM_EPS, ALU.mult, ALU.add)
        yield
        b.recip(smS[:, 4:8], smS[:, 4:8])
        yield
        b.act(smS[:, 4:8], smS[:, 4:8], AF.Sqrt)
        yield
        b.tt(otok[:], otok[:], bc(smS[:, 4:8]), ALU.mult)
        yield
        for h in range(4):
            b.tr(S1[:, hsl(h)], otok[:, h, :], c.ident[:])
        yield
        b.stt(o_a[:, :, gcols], v4(S1), c.ng[:, 0:1], SZ[:, :, cols], ALU.mult, ALU.mult)
        yield

    def drain(g):
        for _ in g:
            pass

    def merge(g1, g2):
        a_alive = b_alive = True
        while a_alive or b_alive:
            if a_alive:
                try:
                    next(g1)
                except StopIteration:
                    a_alive = False
            if b_alive:
                try:
                    next(g2)
                except StopIteration:
                    b_alive = False

    for qi in range(4):
        items = []
        for pair in range(2):
            for kind, base in (("q", 0), ("k", 512), ("v", 1024), ("z", 1544)):
                for hh in range(2):
                    h = pair * 2 + hh
                    it_ = dict(kind=kind, nm=f"{kind}a{h}", h=h, hh=hh, load=(base + pair * 256) if hh == 0 else None,
                               ss=None)
                    if kind == "q":
                        it_.update(acc=qtmps[hh][:], out=QAb[:, h, :], ss=h)
                    elif kind == "k":
                        it_.update(acc=KA[:, h, :], out=KA[:, h, :], ss=4 + h)
                    elif kind == "v":
                        it_.update(acc=VA[:, h, :], out=VA[:, h, :])
                    items.append(it_)
        cur_w = [None]

        def part_a(it_):
            if it_["load"] is not None:
                cur_w[0] = load_pair(it_["load"])
            it_["wt"] = cur_w[0]
            conv_a(it_, qi)

        part_a(items[0])
        for i_, it_ in enumerate(items):
            if i_ + 1 < len(items):
                part_a(items[i_ + 1])
            conv_b(it_, qi)
        rflat = rn_all[:].rearrange("p i t -> p (i t)")
        b.ts(rflat, c.psum[2][:, 0:32], NORM_EPS, None, ALU.add)
        b.recip(rflat, rflat)
        b.act(rflat, rflat, AF.Sqrt)
        nitems = []
        for h in range(4):
            nitems.append((h, QAb[:, h, :], 128.0 ** -0.5, None))
            nitems.append((4 + h, KA[:, h, :], 1.0, KAb[:, h, :]))
        norm_a(nitems[0][0], 0)
        for i_, (item, x_, sc_, xb_) in enumerate(nitems):
            if i_ + 1 < len(nitems):
                norm_a(nitems[i_ + 1][0], i_ + 1)
            norm_b(i_, x_, sc_, xb_)
        drain(front(qi * 4, 0, HO[0]))
        for jl in range(4):
            j = qi * 4 + jl
            if jl + 1 < 4:
                merge(front(j + 1, jl + 1, HO[(jl + 1) % 2]), state(j, jl, HO[jl % 2]))
            else:
                drain(state(j, jl, HO[jl % 2]))


def phase_G(c, l, o_a, o_b, o_c, A, lnidx, defer_last=False):
    s, b, W = c.s, c.b, c.W
    NTOK = 1024
    merged = A("merged", [128, KD, NTOK], BF16)
    wg = [[A(f"wGg{i}{br}", [128, KD, 128], BF16) for br in range(3)] for i in range(2)]
    wbr = [[A(f"wGb{i}{br}", [128, 4, 128], BF16) for br in range(3)] for i in range(2)]
    wo = [A(f"wGo{i}", [128, KD, 128], BF16) for i in range(2)]
    sg = [A(f"sgG{i}", [128, 512], F32) for i in range(2)]
    accb = A("accG", [128, 512], F32)
    ln_off = A.o
    assert ln_off + 2 * KD * 256 * 2 + 2048 <= ARENA_END, ln_off
    wg_v = W["w_gate"][l].rearrange("(k p) c -> p k c", p=128)
    wbr_v = [W[nm][l].rearrange("(f p) c -> p f c", p=128) for nm in ("w_br_a", "w_br_b", "w_br_c")]
    wo_v = W["w_out"][l].rearrange("(k p) c -> p k c", p=128)
    obr = (o_a, o_b, o_c)
    wc = 0
    pc = 0
    pendingG = Pending()
    for st in range(T // NTOK):
        c0 = st * NTOK
        for dc in range(KD):
            i = wc % 2
            wc += 1
            for br in range(3):
                b.dma(wg[i][br][:], wg_v[:, :, br * D + dc * 128:br * D + (dc + 1) * 128], q="pool")
                b.dma(wbr[i][br][:], wbr_v[br][:, :, dc * 128:(dc + 1) * 128], q="pool")
            for nt in range(NTOK // 512):
                cols = slice(c0 + nt * 512, c0 + (nt + 1) * 512)
                for br in range(3):
                    pg = c.psum[pc % 2]
                    pb_ = c.psum[2 + pc % 2]
                    sgb = sg[pc % 2]
                    pc += 1
                    for k in range(KD):
                        b.mm(pg[:], wg[i][br][:, k, :], c.XB[:, k, cols], start=(k == 0), stop=(k == KD - 1))
                    for f in range(4):
                        b.mm(pb_[:], wbr[i][br][:, f, :], obr[br][:, f, cols], start=(f == 0), stop=(f == 3))
                    b.act(sgb[:], pg[:], AF.Sigmoid, bias=c.bgate[:, br * 8 + dc:br * 8 + dc + 1])
                    if br == 0:
                        b.tt(accb[:], sgb[:], pb_[:], ALU.mult)
                    elif br == 1:
                        b.tt(sgb[:], sgb[:], pb_[:], ALU.mult)
                        b.tt(accb[:], accb[:], sgb[:], ALU.add)
                    else:
                        b.tt(sgb[:], sgb[:], pb_[:], ALU.mult)
                        b.tt(merged[:, dc, nt * 512:(nt + 1) * 512], accb[:], sgb[:], ALU.add)
                    pendingG.step(3)
        for dc in range(KD):
            wob = wo[dc % 2]
            b.dma(wob[:], wo_v[:, :, dc * 128:(dc + 1) * 128], q="pool")
            for nt in range(NTOK // 512):
                cols = slice(c0 + nt * 512, c0 + (nt + 1) * 512)
                py = c.psum[4 + pc % 2]
                pc += 1
                for k in range(KD):
                    b.mm(py[:], wob[:, k, :], merged[:, k, nt * 512:(nt + 1) * 512], start=(k == 0), stop=(k == KD - 1))
                xs = c.X32[:, dc, cols]
                b.stt(xs, xs, ALPHA, py[:], ALU.mult, ALU.add)
        pendingG.drain()
        for q in range(NTOK // 256):
            cols = slice(c0 + q * 256, c0 + (q + 1) * 256)
            pendingG.add(layer_norm_gen(c, cols, lnidx, ln_off))
    if defer_last:
        c.deferred = pendingG
    else:
        pendingG.drain()


def mixer_phase(c, l, defer_last=False):
    import os
    load_mixer_layer_consts(c, l)
    A0 = Alloc(c, OFF_ARENA)
    o_b = A0("o_b", [128, 4, T], BF16)
    baseB = A0.o
    o_a = A0("o_a", [128, 4, T], BF16)
    baseA = A0.o
    o_c = A0("o_c", [128, 4, T], BF16)
    baseC = A0.o
    ph = os.environ.get("MIX_PH", "BACG")
    if "B" in ph:
        phase_B(c, l, o_b, Alloc(c, baseB))
    if "A" in ph:
        phase_A(c, l, o_a, Alloc(c, baseA))
    if "C" in ph:
        phase_C(c, l, o_c, Alloc(c, baseC))
    if "G" in ph:
        phase_G(c, l, o_a, o_b, o_c, Alloc(c, baseC), l * 3 + 1, defer_last=defer_last)
```

```python
import math
from contextlib import ExitStack

import numpy as np
import concourse.bass as bass
import concourse.mybir as mybir
from concourse.bass_utils import run_bass_kernel_spmd

F32 = mybir.dt.float32
BF16 = mybir.dt.bfloat16
AF = mybir.ActivationFunctionType
ALU = mybir.AluOpType

D = 1024
T = 2048
DEPTH = 2
F = 4096
KD = D // 128
ALPHA = (2.0 * DEPTH) ** 0.25
LN_EPS = 1e-5
NORM_EPS = 1e-6
D_IN = 4360

COMPUTE = ("pe", "act", "dve", "pool")
ENGS = ("pe", "act", "dve", "pool", "sp")
QUEUES = ("sp", "pool", "act")
NSEM = 12
SB_BASE = 16512
EPOCH = 20000


class Op:
    __slots__ = ("eng", "fn", "dma", "waits", "need_inc", "tick", "slot", "sidx")

    def __init__(self, eng, fn, dma):
        self.eng = eng
        self.fn = fn
        self.dma = dma
        self.waits = []
        self.need_inc = False
        self.tick = 0
        self.slot = 0
        self.sidx = 0


class Sched:
    def __init__(self, nc):
        self.nc = nc
        self.ops = []
        self.hist = {}
        self.tinfo = {}
        self.dma_ops = {q: [] for q in QUEUES}
        self.waited = {e: {} for e in ENGS}
        self.dma_waited = {e: set() for e in ENGS}
        self.npsum = 0

    def sb(self, name, shape, dtype, off):
        es = 4 if dtype == F32 else 2
        row = 1
        for s in shape[1:]:
            row *= s
        assert off % 32 == 0, (name, off)
        assert off + row * es <= 212800, (name, off, row * es)
        t = self.nc.alloc_sbuf_tensor_at(name, list(shape), dtype, offset=SB_BASE + off)
        self.tinfo[t.name] = ("sb", off, row, es)
        return t

    def ps(self, name, shape, dtype=F32):
        row = 1
        for s in shape[1:]:
            row *= s
        es = 4 if dtype == F32 else 2
        t = self.nc.alloc_psum_tensor(name, list(shape), dtype)
        self.tinfo[t.name] = ("ps_" + name, 0, row, es)
        return t

    def region(self, ap):
        name = ap.tensor.name
        info = self.tinfo.get(name)
        if info is None:
            return None
        space, base, row, es = info
        if space.startswith("ps_"):
            return (space, 0, 128, 0, 1 << 30)
        p_lo, f_lo = divmod(ap.offset, row)
        dims = ap.ap
        pstep, pcnt = dims[0]
        p_hi = p_lo + (pcnt - 1) * (pstep // row) + 1 if pstep else p_lo + 1
        span = 0
        for st, c in dims[1:]:
            span += (c - 1) * abs(st)
        return (space, p_lo, p_hi, base + f_lo * es, base + (f_lo + span + 1) * es)

    def add(self, eng, fn, reads=(), writes=(), dma=False):
        idx = len(self.ops)
        op = Op(eng, fn, dma)
        deps = {}
        rregs = [r for r in (self.region(a) for a in reads) if r is not None]
        wregs = [r for r in (self.region(a) for a in writes) if r is not None]
        for reg in rregs:
            for rec in self.hist.get(reg[0], ()):
                if rec[5] and rec[1] < reg[2] and reg[1] < rec[2] and rec[3] < reg[4] and reg[3] < rec[4]:
                    deps[rec[0]] = "raw"
        for reg in wregs:
            for rec in self.hist.get(reg[0], ()):
                if rec[1] < reg[2] and reg[1] < rec[2] and rec[3] < reg[4] and reg[3] < rec[4]:
                    if rec[0] not in deps:
                        deps[rec[0]] = "waw" if rec[5] else "war"
        for reg in wregs:
            lst = self.hist.setdefault(reg[0], [])
            lst[:] = [r for r in lst if not (reg[1] <= r[1] and r[2] <= reg[2] and reg[3] <= r[3] and r[4] <= reg[4])]
            lst.append((idx, reg[1], reg[2], reg[3], reg[4], True, eng, dma))
        for reg in rregs:
            lst = self.hist.setdefault(reg[0], [])
            if not dma:
                lst[:] = [r for r in lst if not ((not r[5]) and r[6] == eng and (not r[7]) and r[1:5] == reg[1:5])]
            lst.append((idx, reg[1], reg[2], reg[3], reg[4], False, eng, dma))
        best = {}
        for d, kind in deps.items():
            p = self.ops[d]
            if p.dma:
                if d not in self.dma_waited[eng]:
                    self.dma_waited[eng].add(d)
                    op.waits.append(d)
                continue
            if not dma and p.eng == eng and eng == "pe":
                continue
            if d > best.get(p.eng, -1):
                best[p.eng] = d
        for pe_, d in best.items():
            if d > self.waited[eng].get(pe_, -1):
                self.waited[eng][pe_] = d
                op.waits.append(d)
                self.ops[d].need_inc = True
        if dma:
            lst = self.dma_ops[eng]
            k = len(lst)
            op.slot = k % NSEM
            op.tick = 16 * (k // NSEM + 1)
            if k >= NSEM:
                prev = lst[k - NSEM]
                if prev not in self.dma_waited[eng]:
                    self.dma_waited[eng].add(prev)
                    op.waits.append(prev)
            lst.append(idx)
        self.ops.append(op)
        return idx

    def emit(self):
        nc = self.nc
        cnt = {e: 0 for e in COMPUTE}
        for op in self.ops:
            if not op.dma and op.need_inc:
                c = cnt[op.eng]
                op.sidx = c // EPOCH
                op.tick = c % EPOCH + 1
                cnt[op.eng] = c + 1
        with ExitStack() as st:
            sems = {e: [st.enter_context(nc.semaphore(f"s_{e}{i}")) for i in range(cnt[e] // EPOCH + 1)]
                    for e in COMPUTE}
            dsems = {q: [st.enter_context(nc.semaphore(f"d_{q}{i}")) for i in range(NSEM)]
                     for q in QUEUES if self.dma_ops[q]}
            block = st.enter_context(nc.Block())
            ops = self.ops

            def run(engname):
                def f(e):
                    for op in ops:
                        if op.eng != engname:
                            continue
                        for d in op.waits:
                            p = ops[d]
                            if p.dma:
                                e.wait_ge(dsems[p.eng][p.slot], p.tick)
                            else:
                                e.wait_ge(sems[p.eng][p.sidx], p.tick)
                        ins = op.fn(e)
                        if op.dma:
                            ins.then_inc(dsems[engname][op.slot], 16)
                        elif op.need_inc:
                            ins.then_inc(sems[engname][op.sidx], 1)
                    if engname in dsems:
                        lst = self.dma_ops[engname]
                        for d in lst[-NSEM:]:
                            p = ops[d]
                            e.wait_ge(dsems[engname][p.slot], p.tick)
                return f

            block.tensor(run("pe"))
            block.scalar(run("act"))
            block.vector(run("dve"))
            block.gpsimd(run("pool"))
            block.sync(run("sp"))


class B:
    def __init__(self, s):
        self.s = s

    def mm(self, out, lhsT, rhs, start, stop, **kw):
        self.s.add("pe", lambda e: e.matmul(out, lhsT, rhs, start=start, stop=stop, **kw),
                   reads=[lhsT, rhs], writes=[out])

    def tr(self, out, in_, ident):
        self.s.add("pe", lambda e: e.transpose(out, in_, ident), reads=[in_, ident], writes=[out])

    def act(self, out, in_, func, bias=None, scale=None, eng="act"):
        reads = [in_]
        kw = {}
        if bias is not None:
            kw["bias"] = bias
            if not isinstance(bias, (int, float)):
                reads.append(bias)
        if scale is not None:
            kw["scale"] = scale
            if not isinstance(scale, (int, float)):
                reads.append(scale)
        self.s.add("act", lambda e: e.activation(out, in_, func, **kw), reads=reads, writes=[out])

    def tt(self, out, in0, in1, op, eng="dve"):
        self.s.add(eng, lambda e: e.tensor_tensor(out, in0, in1, op), reads=[in0, in1], writes=[out])

    def ts(self, out, in0, s1, s2, op0, op1=None, eng="dve"):
        reads = [in0]
        for sc in (s1, s2):
            if sc is not None and not isinstance(sc, (int, float)):
                reads.append(sc)
        if op1 is None:
            self.s.add(eng, lambda e: e.tensor_scalar(out, in0, s1, None, op0), reads=reads, writes=[out])
        else:
            self.s.add(eng, lambda e: e.tensor_scalar(out, in0, s1, s2, op0, op1), reads=reads, writes=[out])

    def stt(self, out, in0, scalar, in1, op0, op1, eng="dve"):
        reads = [in0, in1]
        if not isinstance(scalar, (int, float)):
            reads.append(scalar)
        self.s.add(eng, lambda e: e.scalar_tensor_tensor(out, in0, scalar, in1, op0, op1),
                   reads=reads, writes=[out])

    def copy(self, out, in_, eng="dve"):
        self.s.add(eng, lambda e: e.tensor_copy(out, in_), reads=[in_], writes=[out])

    def memset(self, out, val, eng="dve"):
        self.s.add(eng, lambda e: e.memset(out, val), reads=[], writes=[out])

    def recip(self, out, in_):
        self.s.add("dve", lambda e: e.reciprocal(out, in_), reads=[in_], writes=[out])

    def dma(self, out, in_, q="sp"):
        self.s.add(q, lambda e: e.dma_start(out, in_), reads=[in_], writes=[out], dma=True)


OFF_X32 = 0
OFF_XB = 65536
OFF_CONST = 98304
OFF_ARENA = 106496
ARENA_END = 212800


class Ctx:
    pass


def setup_consts(c):
    s, b = c.s, c.b
    o = OFF_CONST
    c.ident = s.sb("ident", [128, 128], F32, o); o += 512
    c.onesb = s.sb("onesb", [128, 128], BF16, o); o += 256
    c.ones1 = s.sb("ones1", [128, 128], BF16, o); o += 256
    c.lnp = s.sb("lnp", [128, DEPTH * 3 * 2, KD], F32, o); o += DEPTH * 6 * KD * 4
    c.eps = s.sb("eps", [128, 1], F32, o); o += 32
    assert o <= OFF_ARENA
    c.const_end = o
    b.dma(c.ident[:], c.W['ident'], q='sp')
    b.memset(c.onesb[:], 1.0 / 1024.0)
    b.memset(c.ones1[:], 1.0)
    b.memset(c.eps[:], LN_EPS)


def load_ln_params(c, W):
    c.b.dma(c.lnp[:], W["lnp"], q="sp")


def load_x(c, x_dram):
    s, b = c.s, c.b
    stage = [s.sb(f"xstage{i}", [128, D], F32, OFF_ARENA + i * 4096) for i in range(2)]
    for tt in range(getattr(c, 'nload', T // 128)):
        stg = stage[tt % 2]
        b.dma(stg[:], x_dram[tt * 128:(tt + 1) * 128, :], q="sp")
        for half in range(2):
            pt = c.psum[(tt * 2 + half) % 2]
            for j in range(4):
                k = half * 4 + j
                b.tr(pt[:, j * 128:(j + 1) * 128], stg[:, k * 128:(k + 1) * 128], c.ident[:])
            dst32 = c.X32[:, half * 4:(half + 1) * 4, tt * 128:(tt + 1) * 128]
            dstb = c.XB[:, half * 4:(half + 1) * 4, tt * 128:(tt + 1) * 128]
            src = pt[:, :].rearrange("p (j t) -> p j t", j=4)
            b.copy(dst32, src, eng="dve")
            b.act(dstb, dst32, AF.Copy)


def store_x(c, out_dram, tiles=range(16)):
    s, b = c.s, c.b
    stage = [s.sb(f"ostage{i}_{c.uid()}", [128, D], F32, OFF_ARENA + i * 4096) for i in range(2)]
    for tt in tiles:
        stg = stage[tt % 2]
        for half in range(2):
            pt = c.psum[(tt * 2 + half) % 2]
            for j in range(4):
                k = half * 4 + j
                b.tr(pt[:, j * 128:(j + 1) * 128], c.X32[:, k, tt * 128:(tt + 1) * 128], c.ident[:])
            if half == 0:
                b.copy(stg[:, 0:512], pt[:, :], eng="dve")
            else:
                b.act(stg[:, 512:1024], pt[:, :], AF.Copy)
        b.dma(out_dram[tt * 128:(tt + 1) * 128, :], stg[:], q="sp")


def layer_norm_gen(c, cols, lnidx, scratch_off):
    s, b = c.s, c.b
    n = cols.stop - cols.start
    o = scratch_off
    rb = s.sb(f"ln_rb{c.uid()}", [128, KD, n], BF16, o); o += KD * n * 2
    rsq = s.sb(f"ln_rsq{c.uid()}", [128, KD, n], BF16, o); o += KD * n * 2
    mean = s.sb(f"ln_mean{c.uid()}", [128, n], F32, o); o += n * 4
    rstd = s.sb(f"ln_rstd{c.uid()}", [128, n], F32, o); o += n * 4
    s1 = c.psum[6]
    s2 = c.psum[7]
    for k in range(KD):
        b.act(rb[:, k, :], c.X32[:, k, cols], AF.Copy)
        b.act(rsq[:, k, :], c.X32[:, k, cols], AF.Square)
        yield
    for k in range(KD):
        b.mm(s1[:, 0:n], c.onesb[:], rb[:, k, :], start=(k == 0), stop=(k == KD - 1))
    yield
    for k in range(KD):
        b.mm(s2[:, 0:n], c.onesb[:], rsq[:, k, :], start=(k == 0), stop=(k == KD - 1))
    yield
    b.copy(mean[:], s1[:, 0:n])
    b.tt(rstd[:], mean[:], mean[:], ALU.mult)
    yield
    b.tt(rstd[:], s2[:, 0:n], rstd[:], ALU.subtract)
    b.ts(rstd[:], rstd[:], LN_EPS, None, ALU.add)
    yield
    b.recip(rstd[:], rstd[:])
    yield
    b.act(rstd[:], rstd[:], AF.Sqrt)
    yield
    gi = lnidx * 2
    xv = c.X32[:, :, cols]
    b.tt(xv, xv, mean[:].unsqueeze(1).to_broadcast([128, KD, n]), ALU.subtract)
    yield
    b.tt(xv, xv, rstd[:].unsqueeze(1).to_broadcast([128, KD, n]), ALU.mult)
    yield
    for k in range(KD):
        xs = c.X32[:, k, cols]
        b.act(c.XB[:, k, cols], xs, AF.Identity, bias=c.lnp[:, gi + 1, k:k + 1], scale=c.lnp[:, gi, k:k + 1])
        b.act(xs, xs, AF.Identity, bias=c.lnp[:, gi + 1, k:k + 1], scale=c.lnp[:, gi, k:k + 1])
        yield


def layer_norm_tile(c, cols, lnidx, scratch_off):
    for _ in layer_norm_gen(c, cols, lnidx, scratch_off):
        pass


class Pending:
    def __init__(self):
        self.gens = []

    def add(self, g):
        self.gens.append(g)

    def step(self, n=1):
        for _ in range(n):
            while self.gens:
                try:
                    next(self.gens[0])
                    break
                except StopIteration:
                    self.gens.pop(0)

    def drain(self):
        while self.gens:
            self.step()


def ffn_phase(c, w_in, w_out, lnidx, defer_last=False):
    s, b = c.s, c.b
    o = OFF_ARENA
    NTOK = 1024
    g = s.sb(f"ffn_g{c.uid()}", [128, 16, NTOK], BF16, o); o += 16 * NTOK * 2
    wg = []
    wu = []
    for i in range(3):
        wg.append(s.sb(f"ffn_wg{i}_{c.uid()}", [128, KD, 256], BF16, o)); o += KD * 256 * 2
        wu.append(s.sb(f"ffn_wu{i}_{c.uid()}", [128, KD, 256], BF16, o)); o += KD * 256 * 2
    wo = []
    for i in range(2):
        wo.append(s.sb(f"ffn_wo{i}_{c.uid()}", [128, 16, 128], BF16, o)); o += 16 * 128 * 2
    sg = []
    for i in range(2):
        sg.append(s.sb(f"ffn_sg{i}_{c.uid()}", [128, 512], F32, o)); o += 2048
    ln_off = o
    assert ln_off + 2 * KD * 512 * 2 + 4096 <= ARENA_END, ln_off
    w_in_v = w_in.rearrange("(k p) c -> p k c", p=128)
    w_out_v = w_out.rearrange("(f p) c -> p f c", p=128)
    wcnt = 0
    ocnt = 0
    pcnt = 0
    pending = getattr(c, "deferred", None) or Pending()
    c.deferred = None
    for st in range(T // NTOK):
        c0 = st * NTOK
        for hh in range(2):
            for fp in range(8):
                fc0 = hh * 16 + fp * 2
                wgb = wg[wcnt % 3]
                wub = wu[wcnt % 3]
                wcnt += 1
                b.dma(wgb[:], w_in_v[:, :, fc0 * 128:fc0 * 128 + 256], q="pool")
                b.dma(wub[:], w_in_v[:, :, F + fc0 * 128:F + fc0 * 128 + 256], q="pool")
                for j in range(2):
                    fl = fp * 2 + j
                    for nt in range(NTOK // 512):
                        cols = slice(c0 + nt * 512, c0 + (nt + 1) * 512)
                        pg = c.psum[0 + pcnt % 2]
                        pu = c.psum[2 + pcnt % 2]
                        sgb = sg[pcnt % 2]
                        pcnt += 1
                        for k in range(KD):
                            b.mm(pg[:], wgb[:, k, j * 128:(j + 1) * 128], c.XB[:, k, cols],
                                 start=(k == 0), stop=(k == KD - 1))
                        for k in range(KD):
                            b.mm(pu[:], wub[:, k, j * 128:(j + 1) * 128], c.XB[:, k, cols],
                                 start=(k == 0), stop=(k == KD - 1))
                        b.act(sgb[:], pg[:], AF.Silu)
                        b.stt(g[:, fl, nt * 512:(nt + 1) * 512], pu[:], 0.5, sgb[:], ALU.mult, ALU.mult)
                        pending.step(2)
            for dc in range(KD):
                wob = wo[ocnt % 2]
                ocnt += 1
                b.dma(wob[:], w_out_v[:, hh * 16:(hh + 1) * 16, dc * 128:(dc + 1) * 128], q="pool")
                for nt in range(NTOK // 512):
                    cols = slice(c0 + nt * 512, c0 + (nt + 1) * 512)
                    py = c.psum[4 + pcnt % 2]
                    pcnt += 1
                    for f in range(16):
                        b.mm(py[:], wob[:, f, :], g[:, f, nt * 512:(nt + 1) * 512],
                             start=(f == 0), stop=(f == 15))
                    xs = c.X32[:, dc, cols]
                    if hh == 0:
                        b.stt(xs, xs, ALPHA, py[:], ALU.mult, ALU.add)
                    else:
                        b.tt(xs, xs, py[:], ALU.add)
                    pending.step(1)
        pending.drain()
        for nt in range(NTOK // 512):
            cols = slice(c0 + nt * 512, c0 + (nt + 1) * 512)
            pending.add(layer_norm_gen(c, cols, lnidx, ln_off))
    if defer_last:
        c.deferred = pending
    else:
        pending.drain()


def flush_deferred(c):
    p = getattr(c, "deferred", None)
    if p is not None:
        p.drain()
    c.deferred = None


def build_program(phases=("ffn1", "mix", "ffn2"), layers=(0, 1)):
    nc = bass.Bass("TRN2", target_bir_lowering=False)
    W = {}

    def inp(name, shape):
        W[name] = nc.dram_tensor(name, list(shape), F32, kind="ExternalInput").ap()

    inp("x", [T, D])
    inp("ident", [128, 128])
    inp("lnp", [128, DEPTH * 6, KD])
    if "ffn1" in phases:
        inp("w_ff1_in", [DEPTH, D, 2 * F])
        inp("w_ff1_out", [DEPTH, F, D])
    if "ffn2" in phases:
        inp("w_ff2_in", [DEPTH, D, 2 * F])
        inp("w_ff2_out", [DEPTH, F, D])
    if "mix" in phases:
        inp("w_in", [DEPTH, D, D_IN])
        inp("b_in", [DEPTH, D_IN])
        inp("w_gate", [DEPTH, D, 3 * D])
        for nm in ("w_br_a", "w_br_b", "w_br_c"):
            inp(nm, [DEPTH, 512, D])
        inp("w_out", [DEPTH, D, D])
        inp("a_log", [DEPTH, 4])
        inp("dt_bias", [DEPTH, 4])
        inp("sinks", [DEPTH, 8])
        inp("MU", [128, 128]); inp("MUd", [128, 128]); inp("SC", [128, 128])
        inp("bfm", [DEPTH, 128, NFM]); inp("bgate_fm", [DEPTH, 128, 24]); inp("convw_fm", [DEPTH, 128, 12, 4])
        inp("ng_fm", [DEPTH, 128, 1])
        inp("BMg", [DEPTH, 128, 4, 640]); inp("MaskB", [128, 640]); inp("AMc", [128, 8, 2, 128])
    out = nc.dram_tensor("out", [T, D], F32, kind="ExternalOutput").ap()

    s = Sched(nc)
    c = Ctx()
    c.s = s
    c.b = B(s)
    c.nc = nc
    c.W = W
    c._uid = 0

    def uid():
        c._uid += 1
        return c._uid
    c.uid = uid
    c.X32 = s.sb("X32", [128, KD, T], F32, OFF_X32)
    c.XB = s.sb("XB", [128, KD, T], BF16, OFF_XB)
    c.psum = [s.ps(f"ps{i}", [128, 512], F32) for i in range(8)]
    setup_consts(c)
    load_ln_params(c, W)
    if "mix" in phases:
        setup_mixer_consts(c)
    load_x(c, W["x"])
    for l in layers:
        if "ffn1" in phases:
            ffn_phase(c, W["w_ff1_in"][l], W["w_ff1_out"][l], l * 3 + 0)
        if "mix" in phases:
            flush_deferred(c)
            mixer_phase(c, l, defer_last=("ffn2" in phases))
        if "ffn2" in phases:
            ffn_phase(c, W["w_ff2_in"][l], W["w_ff2_out"][l], l * 3 + 2, defer_last=True)
            if not (l + 1 < DEPTH and l + 1 in layers and "ffn1" in phases) and l != layers[-1]:
                flush_deferred(c)
    store_x(c, out, tiles=range(0, 8))
    flush_deferred(c)
    store_x(c, out, tiles=range(8, 16))
    s.emit()
    return nc, list(W.keys())


def host_consts(inputs):
    h = {}
    h["ident"] = np.eye(128, dtype=np.float32)
    names = [("ln1_g", "ln1_b"), ("ln2_g", "ln2_b"), ("ln3_g", "ln3_b")]
    lnp = np.zeros((128, DEPTH * 6, KD), np.float32)
    for l in range(DEPTH):
        for j, pair in enumerate(names):
            for q, nm in enumerate(pair):
                lnp[:, (l * 3 + j) * 2 + q, :] = np.asarray(inputs[nm][l], np.float32).reshape(KD, 128).T
    h["lnp"] = lnp
    idx = np.arange(128)
    same = (idx[:, None] // 64) == (idx[None, :] // 64)
    h["MU"] = ((idx[None, :] > idx[:, None]) & same).astype(np.float32)
    h["MUd"] = ((idx[None, :] >= idx[:, None]) & same).astype(np.float32)
    h["SC"] = same.astype(np.float32)
    if "b_in" not in inputs:
        return h
    b_in = np.asarray(inputs["b_in"], np.float32)
    bfm = np.zeros((DEPTH, 128, NFM), np.float32)
    for i, (nm, c0) in enumerate(FM_CHUNKS):
        bfm[:, :, i] = b_in[:, c0:c0 + 128]
    for kv in range(2):
        seg = b_in[:, 4104 + kv * 64:4104 + (kv + 1) * 64]
        bfm[:, :, len(FM_CHUNKS) + kv] = np.concatenate([seg, seg], axis=1)
    h["bfm"] = bfm
    h["bgate_fm"] = np.ascontiguousarray(
        np.asarray(inputs["b_gate"], np.float32).reshape(DEPTH, 24, 128).transpose(0, 2, 1))
    h["convw_fm"] = np.ascontiguousarray(
        np.asarray(inputs["conv_w"], np.float32).reshape(DEPTH, 4, 12, 128).transpose(0, 3, 2, 1))
    h["ng_fm"] = np.asarray(inputs["gdn_norm_g"], np.float32).reshape(DEPTH, 128, 1).copy()
    r = np.arange(128)[:, None, None]
    a = np.arange(5)[None, :, None]
    sq = np.arange(128)[None, None, :]
    dist = 128 * (4 - a) + sq - r
    ridx = np.clip(dist, -63, 128) + 63
    rb = np.asarray(inputs["rel_bias"], np.float32)
    h["BMg"] = np.ascontiguousarray(rb[:, :, ridx].transpose(0, 2, 1, 3, 4).reshape(DEPTH, 128, 4, 640))
    kk = 128 * (a - 4) + r
    dch = (sq // 64) - np.floor_divide(kk, 64)
    vis = (dch >= 0) & (dch <= 8)
    h["MaskB"] = np.where(vis, 0.0, -30000.0).astype(np.float32).reshape(128, 640)
    a2 = np.arange(2)[None, :, None]
    kk2 = 128 * (a2 - 1) + r
    dist2 = sq - kk2
    dch2 = (sq // 64) - np.floor_divide(kk2, 64)
    vis2 = (dch2 >= 0) & (dch2 <= 2)
    slopes = (2.0 ** (-8.0 * np.arange(1, 9, dtype=np.float32) / 8)).astype(np.float32)
    am = np.where(vis2[None], -slopes[:, None, None, None] * np.abs(dist2)[None].astype(np.float32), -30000.0)
    h["AMc"] = np.ascontiguousarray(am.transpose(1, 0, 2, 3)).astype(np.float32)
    return h


def make_in_maps(inputs, names, ncores=8):
    h = host_consts(inputs)
    shared = {}
    for n in names:
        if n == "x":
            continue
        shared[n] = h[n] if n in h else np.ascontiguousarray(inputs[n], dtype=np.float32)
    x = np.ascontiguousarray(inputs["x"], dtype=np.float32)
    in_maps = []
    for i in range(ncores):
        m = dict(shared)
        m["x"] = x[i]
        in_maps.append(m)
    return in_maps


def kernel(**inputs):
    nc, names = build_program()
    in_maps = make_in_maps(inputs, names)
    res = run_bass_kernel_spmd(nc, in_maps, core_ids=list(range(8)))
    return np.stack([r["out"] for r in res.results], axis=0)


FM_CHUNKS = ([("qa%d" % h, h * 128) for h in range(4)] + [("ka%d" % h, 512 + h * 128) for h in range(4)]
             + [("va%d" % h, 1024 + h * 128) for h in range(4)] + [("za%d" % h, 1544 + h * 128) for h in range(4)]
             + [("qb%d" % h, 2056 + h * 128) for h in range(4)] + [("kb%d" % h, 2568 + h * 128) for h in range(4)]
             + [("qc%d" % j, 3592 + j * 128) for j in range(4)])
FM_IDX = {nm: i for i, (nm, _) in enumerate(FM_CHUNKS)}
FM_COL = {nm: c0 for nm, c0 in FM_CHUNKS}
NFM = len(FM_CHUNKS) + 2
OFF_MCONST = OFF_CONST + 2048


def setup_mixer_consts(c):
    s, b = c.s, c.b
    o = OFF_MCONST
    c.ones32 = s.sb("ones32", [128, 128], F32, o); o += 512
    c.MU = s.sb("MU", [128, 128], F32, o); o += 512
    c.MUd = s.sb("MUd", [128, 128], F32, o); o += 512
    c.SC = s.sb("SC", [128, 128], F32, o); o += 512
    c.bfm = s.sb("bfm", [128, NFM], F32, o); o += 160
    c.bgate = s.sb("bgate", [128, 24], F32, o); o += 96
    c.convw = s.sb("convw", [128, 12, 4], F32, o); o += 192
    c.ng = s.sb("ng", [128, 1], F32, o); o += 32
    c.esink = s.sb("esink", [128, 8], F32, o); o += 32
    c.nA = s.sb("nA", [128, 4], F32, o); o += 32
    c.dtb = s.sb("dtb", [128, 4], F32, o); o += 32
    c.brow_ba = s.sb("brow_ba", [128, 8], F32, o); o += 32
    c.identb = s.sb("identb", [128, 128], BF16, o); o += 256
    assert o <= OFF_ARENA, o
    b.memset(c.ones32[:], 1.0)
    b.copy(c.identb[:], c.ident[:])
    b.dma(c.MU[:], c.W["MU"], q="sp")
    b.dma(c.MUd[:], c.W["MUd"], q="sp")
    b.dma(c.SC[:], c.W["SC"], q="sp")


def load_mixer_layer_consts(c, l):
    b, W = c.b, c.W
    b.dma(c.bfm[:], W["bfm"][l], q="sp")
    b.dma(c.bgate[:], W["bgate_fm"][l], q="sp")
    b.dma(c.convw[:], W["convw_fm"][l], q="sp")
    b.dma(c.ng[:], W["ng_fm"][l], q="sp")
    b.dma(c.esink[:], W["sinks"][l:l + 1, :].partition_broadcast(128), q="sp")
    b.act(c.esink[:], c.esink[:], AF.Exp)
    b.dma(c.nA[:], W["a_log"][l:l + 1, :].partition_broadcast(128), q="sp")
    b.act(c.nA[:], c.nA[:], AF.Exp)
    b.ts(c.nA[:], c.nA[:], -1.0, None, ALU.mult)
    b.dma(c.dtb[:], W["dt_bias"][l:l + 1, :].partition_broadcast(128), q="sp")
    b.dma(c.brow_ba[:], W["b_in"][l:l + 1, 1536:1544].partition_broadcast(128), q="sp")


class Alloc:
    def __init__(self, c, start):
        self.c = c
        self.o = start

    def __call__(self, name, shape, dtype):
        es = 4 if dtype == F32 else 2
        row = 1
        for x_ in shape[1:]:
            row *= x_
        nbytes = (row * es + 31) // 32 * 32
        t = self.c.s.sb(f"{name}_{self.c.uid()}", shape, dtype, self.o)
        self.o += nbytes
        assert self.o <= ARENA_END, (name, self.o)
        return t


def proj_fm(c, w_in_l, wbufs, cnt, col0, evac, dup64=False, nts=(0, 1, 2, 3)):
    b = c.b
    w_v = w_in_l.rearrange("(k p) c -> p k c", p=128)
    if len(cnt) < 2:
        cnt.append(0)
    wt = wbufs[cnt[1] % len(wbufs)]
    if dup64:
        b.dma(wt[:, :, 0:64], w_v[:, :, col0:col0 + 64], q="pool")
        b.dma(wt[:, :, 64:128], w_v[:, :, col0:col0 + 64], q="pool")
    else:
        b.dma(wt[:], w_v[:, :, col0:col0 + 128], q="pool")
    for nt in nts:
        ps = c.psum[cnt[0] % 2]
        cnt[0] += 1
        for k in range(KD):
            b.mm(ps[:], wt[:, k, :], c.XB[:, k, nt * 512:(nt + 1) * 512], start=(k == 0), stop=(k == KD - 1))
        evac(nt, ps)
    cnt[1] = cnt[1] + 1 if len(cnt) > 1 else 0


def phase_B(c, l, o_b, A):
    s, b, W = c.s, c.b, c.W
    w_in_l = W["w_in"][l]
    qb = A("qb", [128, 4, T], BF16)
    kb = A("kb", [128, 4, T], BF16)
    vb = A("vb", [128, 16, 512], BF16)
    wb = [A(f"wB{i}", [128, KD, 128], BF16) for i in range(2)]
    wv = A("wBv", [128, KD, 512], BF16)
    brow = A("browB", [128, 512], F32)
    BM = A("BM", [128, 4, 640], F32)
    MB = A("MB", [128, 640], F32)
    tt_ = [A(f"tB{i}", [128, 640], F32) for i in range(2)]
    pT = [A(f"pB{i}", [128, 640], BF16) for i in range(2)]
    rden = [A(f"rdB{i}", [128, 128], F32) for i in range(2)]
    cnt = [0]
    for h in range(4):
        for nm, dst in ((f"qb{h}", qb), (f"kb{h}", kb)):
            bi = FM_IDX[nm]
            proj_fm(c, w_in_l, wb, cnt, FM_COL[nm],
                    lambda nt, ps, dst=dst, h=h, bi=bi: b.act(dst[:, h, nt * 512:(nt + 1) * 512], ps[:], AF.Identity,
                                                              bias=c.bfm[:, bi:bi + 1]))
    w_v = w_in_l.rearrange("(k p) c -> p k c", p=128)
    b.dma(wv[:], w_v[:, :, 3080:3592], q="pool")
    b.dma(brow[:], W["b_in"][l:l + 1, 3080:3592].partition_broadcast(128), q="sp")
    for tile in range(16):
        ps = c.psum[tile % 2]
        for k in range(KD):
            b.mm(ps[:], c.XB[:, k, tile * 128:(tile + 1) * 128], wv[:, k, :], start=(k == 0), stop=(k == KD - 1))
        b.tt(vb[:, tile, :], ps[:], brow[:], ALU.add)
    b.dma(BM[:], W["BMg"][l], q="sp")
    b.dma(MB[:], W["MaskB"], q="sp")
    for h in range(4):
        b.tt(BM[:, h, :], BM[:, h, :], MB[:], ALU.add)
    scale = 128.0 ** -0.5

    def bufs(it):
        return (c.psum[2 + (it % 2) * 3], c.psum[3 + (it % 2) * 3], c.psum[4 + (it % 2) * 3],
                tt_[it % 2], pT[it % 2], rden[it % 2])

    def stage1(it, h, n):
        pa, pb_, po, t_, p_, rd = bufs(it)
        a0 = max(0, 4 - n)
        for a in range(a0, 5):
            m = n - 4 + a
            dst = pa[:, a * 128:(a + 1) * 128] if a < 4 else pb_[:, 0:128]
            b.mm(dst, kb[:, h, m * 128:(m + 1) * 128], qb[:, h, n * 128:(n + 1) * 128], start=True, stop=True)
        if a0 < 4:
            b.stt(t_[:, a0 * 128:512], pa[:, a0 * 128:512], scale, BM[:, h, a0 * 128:512], ALU.mult, ALU.add)
        b.stt(t_[:, 512:640], pb_[:, 0:128], scale, BM[:, h, 512:640], ALU.mult, ALU.add)
        b.act(p_[:, a0 * 128:640], t_[:, a0 * 128:640], AF.Exp)

    def stage2(it, h, n):
        pa, pb_, po, t_, p_, rd = bufs(it)
        a0 = max(0, 4 - n)
        for a in range(a0, 5):
            m = n - 4 + a
            b.mm(po[:, 0:128], vb[:, m, h * 128:(h + 1) * 128], p_[:, a * 128:(a + 1) * 128],
                 start=(a == a0), stop=(a == 4))
        for a in range(a0, 5):
            b.mm(po[:, 128:256], c.ones1[:], p_[:, a * 128:(a + 1) * 128], start=(a == a0), stop=(a == 4))
        b.recip(rd[:], po[:, 128:256])
        b.tt(o_b[:, h, n * 128:(n + 1) * 128], po[:, 0:128], rd[:], ALU.mult)

    iters = [(h, n) for h in range(4) for n in range(16)]
    stage1(0, *iters[0])
    for i, (h, n) in enumerate(iters):
        if i + 1 < len(iters):
            stage1(i + 1, *iters[i + 1])
        stage2(i, h, n)


def phase_C(c, l, o_c, A):
    s, b, W = c.s, c.b, c.W
    w_in_l = W["w_in"][l]
    w_v = w_in_l.rearrange("(k p) c -> p k c", p=128)
    qc = A("qc", [128, 2, T], BF16)
    kc = [A("kc0", [128, T], BF16), A("kc1", [128, T], BF16)]
    vc = A("vc", [128, 16, 128], BF16)
    wb = [A(f"wC{i}", [128, KD, 128], BF16) for i in range(2)]
    wv = A("wCv", [128, KD, 128], BF16)
    brow = A("browC", [128, 128], F32)
    AM = A("AM", [128, 8, 2, 128], F32)
    tt_ = [A(f"tC{i}", [128, 512], F32) for i in range(4)]
    pT = [A(f"pC{i}", [128, 512], BF16) for i in range(4)]
    dsb = [A(f"dC{i}", [128, 512], F32) for i in range(2)]
    cnt = [0]
    b.dma(AM[:], W["AMc"], q="sp")
    it = 0
    for kv in range(2):
        for jj in range(2):
            j = kv * 2 + jj
            bi = FM_IDX[f"qc{j}"]
            proj_fm(c, w_in_l, wb, cnt, FM_COL[f"qc{j}"],
                    lambda nt, ps, jj=jj, bi=bi: b.act(qc[:, jj, nt * 512:(nt + 1) * 512], ps[:], AF.Identity,
                                                       bias=c.bfm[:, bi:bi + 1]))
        bi = len(FM_CHUNKS) + kv

        def ev(nt, ps, bi=bi):
            cs_ = slice(nt * 512, (nt + 1) * 512)
            b.act(kc[0][:, cs_], ps[:], AF.Identity, bias=c.bfm[:, bi:bi + 1])
            b.copy(kc[1][:, cs_], kc[0][:, cs_])
            b.memset(kc[0][64:128, cs_], 0.0)
            b.memset(kc[1][0:64, cs_], 0.0)
        proj_fm(c, w_in_l, wb, cnt, 4104 + kv * 64, ev, dup64=True)
        for q2 in range(2):
            b.dma(wv[:, :, q2 * 64:(q2 + 1) * 64], w_v[:, :, 4232 + kv * 64:4232 + (kv + 1) * 64], q="pool")
            b.dma(brow[:, q2 * 64:(q2 + 1) * 64],
                  W["b_in"][l:l + 1, 4232 + kv * 64:4232 + (kv + 1) * 64].partition_broadcast(128), q="sp")
        for tile in range(16):
            ps = c.psum[tile % 2]
            for k in range(KD):
                b.mm(ps[:, 0:128], c.XB[:, k, tile * 128:(tile + 1) * 128], wv[:, k, :],
                     start=(k == 0), stop=(k == KD - 1))
            b.tt(vc[:, tile, :], ps[:, 0:128], brow[:], ALU.add)
        def cbufs(it):
            return ([c.psum[2 + (it % 2) * 3], c.psum[3 + (it % 2) * 3]], c.psum[4 + (it % 2) * 3], dsb[it % 2])

        def cstage1(it, n, kv=kv):
            pss, po, ds_ = cbufs(it)
            a0 = 1 if n == 0 else 0
            for a in range(a0, 2):
                m = n - 1 + a
                ps_a = pss[a]
                for g in range(4):
                    jj, half = g // 2, g % 2
                    b.mm(ps_a[:, g * 128:(g + 1) * 128], kc[half][:, m * 128:(m + 1) * 128],
                         qc[:, jj, n * 128:(n + 1) * 128], start=True, stop=True)
                t_ = tt_[(it % 2) * 2 + a]
                p_ = pT[(it % 2) * 2 + a]
                b.stt(t_[:].rearrange("p (g q) -> p g q", g=4), ps_a[:].rearrange("p (g q) -> p g q", g=4), 0.125,
                      AM[:, kv * 4:(kv + 1) * 4, a, :], ALU.mult, ALU.add)
                b.act(p_[:], t_[:], AF.Exp)

        def cstage2(it, n, kv=kv):
            pss, po, ds_ = cbufs(it)
            pd = pss[1]
            a0 = 1 if n == 0 else 0
            for a in range(a0, 2):
                m = n - 1 + a
                b.mm(po[:], vc[:, m, :], pT[(it % 2) * 2 + a][:], start=(a == a0), stop=(a == 1))
            for a in range(a0, 2):
                b.mm(pd[:], c.ones1[:], pT[(it % 2) * 2 + a][:], start=(a == a0), stop=(a == 1))
            b.tt(ds_[:].rearrange("p (g q) -> p g q", g=4), pd[:].rearrange("p (g q) -> p g q", g=4),
                 c.esink[:, kv * 4:(kv + 1) * 4].unsqueeze(2).to_broadcast([128, 4, 128]), ALU.add)
            b.recip(ds_[:], ds_[:])
            for g in range(4):
                hq = kv * 4 + g
                j, half = hq // 2, hq % 2
                pr = slice(half * 64, half * 64 + 64)
                b.tt(o_c[pr, j, n * 128:(n + 1) * 128], po[pr, g * 128:(g + 1) * 128], ds_[pr, g * 128:(g + 1) * 128],
                     ALU.mult)

        cstage1(0, 0)
        for n in range(16):
            if n + 1 < 16:
                cstage1(n + 1, n + 1)
            cstage2(n, n)


def phase_A(c, l, o_a, A):
    s, b, W = c.s, c.b, c.W
    w_in_l = W["w_in"][l]
    w_v = w_in_l.rearrange("(k p) c -> p k c", p=128)
    Q = 512
    wb = [A(f"wA{i}", [128, KD, 128], BF16) for i in range(2)]
    wba = A("wAba", [128, KD, 8], BF16)
    ba = A("ba", [128, 16, 8], F32)
    beta = A("beta", [128, 16, 4], F32)
    gg = A("gg", [128, 16, 4], F32)
    KA = A("KA", [128, 4, Q], F32)
    VA = A("VA", [128, 4, Q], F32)
    QAb = A("QAb", [128, 4, Q], BF16)
    KAb = A("KAb", [128, 4, Q], BF16)
    SZ = A("SZ", [128, 4, Q], BF16)
    S4 = A("S4", [128, 4, 128], F32)
    S4b = A("S4b", [128, 4, 128], BF16)
    carry = A("carry", [128, 12, 4], F32)
    sm = A("smA", [128, 16], F32)
    smS = A("smS", [128, 16], F32)
    rn_all = A("rn_all", [128, 8, 4], F32)
    slot_off = A.o
    Gm, GCrow, E, EMb, EMd, ktok, EGrow = [A(f"fslot{i}", [128, 4, 128], F32) for i in range(7)]
    Rw, Up, UpT = [A(f"bslot{i}", [128, 4, 128], BF16) for i in range(3)]
    HO = []
    for i in range(2):
        HO.append(dict(
            eglast=A(f"eglast{i}", [128, 4, 2], F32),
            vtok=A(f"vtok{i}", [128, 4, 128], BF16), kdec0=A(f"kdec0{i}", [128, 4, 128], BF16),
            kdec1=A(f"kdec1{i}", [128, 4, 128], BF16), Xf=A(f"Xf{i}", [128, 4, 128], BF16),
            nw0T=A(f"nw0T{i}", [128, 4, 128], BF16), attnT=A(f"attnT{i}", [128, 4, 128], BF16),
            qdec=A(f"qdec{i}", [128, 4, 128], BF16)))
    vnew = A("vnew", [128, 4, 128], BF16)
    otok = A("otok", [128, 4, 128], F32)
    junk = A("junk", [128, 4, 128], BF16)
    u = c.uid()
    rawps = [s.sb(f"rawpA{i}{u}", [128, 3 + Q], F32, slot_off + i * 2080) for i in range(2)]
    sqbs = [s.sb(f"sqbA{i}{u}", [128, Q], BF16, slot_off + 4160 + i * 1024) for i in range(2)]
    qtmps = [s.sb(f"qtmpA{i}{u}", [128, Q], F32, slot_off + 6208 + i * 2048) for i in range(2)]
    diags = [s.sb(f"diagA0{u}", [128, 4, 128], F32, slot_off + 10304),
             s.sb(f"diagA1{u}", [128, 4, 128], F32, slot_off)]
    W2 = [s.sb(f"wA2a{u}", [128, KD, 256], BF16, s.tinfo[wb[0].name][1]),
          s.sb(f"wA2b{u}", [128, KD, 256], BF16, slot_off + 12352)]
    wcn = [0]
    itc = [0]
    P = s.sb(f"gP{u}", [128, 4, 128], BF16, slot_off + 2048)
    PT = s.sb(f"gPT{u}", [128, 4, 128], BF16, slot_off + 2048 + 1024)
    X = s.sb(f"gX{u}", [128, 4, 128], BF16, slot_off + 4096)
    X2 = s.sb(f"gX2{u}", [128, 4, 128], BF16, slot_off + 4096 + 1024)

    def bc(ap4, np_=128):
        return ap4.unsqueeze(2).to_broadcast([np_, 4, 128])

    def hb(ap2):
        return ap2.unsqueeze(1).to_broadcast([128, 4, 128])

    def flat(t4):
        return t4[:].rearrange("p h n -> p (h n)")

    cnt = [0]
    for i in range(2):
        b.memset(HO[i]["kdec0"][:], 0.0)
        b.memset(HO[i]["kdec1"][:], 0.0)
    b.memset(S4[:], 0.0)
    b.memset(S4b[:], 0.0)
    b.memset(vnew[:], 0.0)
    b.dma(wba[:], w_v[:, :, 1536:1544], q="pool")
    for tile in range(16):
        ps = c.psum[tile % 2]
        for k in range(KD):
            b.mm(ps[:, 0:8], c.XB[:, k, tile * 128:(tile + 1) * 128], wba[:, k, :], start=(k == 0), stop=(k == KD - 1))
        b.tt(ba[:, tile, :], ps[:, 0:8], c.brow_ba[:], ALU.add)
    b.act(beta[:], ba[:, :, 0:4], AF.Sigmoid)
    b.tt(gg[:], ba[:, :, 4:8], c.dtb[:].unsqueeze(1).to_broadcast([128, 16, 4]), ALU.add)
    b.act(gg[:], gg[:], AF.Exp)
    b.act(gg[:], gg[:], AF.Ln, bias=1.0)
    b.tt(gg[:], gg[:], c.nA[:].unsqueeze(1).to_broadcast([128, 16, 4]), ALU.mult)

    def proj_q(wt, hh, qi, evac):
        ps = c.psum[cnt[0] % 2]
        cnt[0] += 1
        for k in range(KD):
            b.mm(ps[:], wt[:, k, hh * 128:(hh + 1) * 128], c.XB[:, k, qi * 512:(qi + 1) * 512],
                 start=(k == 0), stop=(k == KD - 1))
        evac(ps)

    def conv_a(it_, qi):
        nm = it_["nm"]
        ci = FM_COL[nm] // 128
        bi = FM_IDX[nm]
        if it_["kind"] == "z":
            h = it_["h"]
            proj_q(it_["wt"], it_["hh"], qi,
                   lambda ps: b.act(SZ[:, h, :], ps[:], AF.Silu, bias=c.bfm[:, bi:bi + 1]))
            return
        cr = carry[:, ci, 0:3]
        rawp = rawps[itc[0] % 2]
        itc[0] += 1
        it_["rawp"] = rawp
        if qi == 0:
            b.memset(rawp[:, 0:3], 0.0)
        else:
            b.copy(rawp[:, 0:3], cr)
        proj_q(it_["wt"], it_["hh"], qi,
               lambda ps: b.act(rawp[:, 3:3 + Q], ps[:], AF.Identity, bias=c.bfm[:, bi:bi + 1]))

    def conv_b(it_, qi):
        flush_ssq()
        if it_["kind"] == "z":
            return
        ci = FM_COL[it_["nm"]] // 128
        rawp, acc_, out_ = it_["rawp"], it_["acc"], it_["out"]
        b.ts(acc_, rawp[:, 0:Q], c.convw[:, ci, 0:1], None, ALU.mult)
        for j in range(1, 4):
            b.stt(acc_, rawp[:, j:j + Q], c.convw[:, ci, j:j + 1], acc_, ALU.mult, ALU.add)
        b.copy(carry[:, ci, 0:3], rawp[:, Q:Q + 3])
        b.act(out_, acc_, AF.Silu)
        if it_["ss"] is not None:
            sumsq(out_, it_["ss"])

    def load_pair(col0):
        wt = W2[wcn[0] % 2]
        wcn[0] += 1
        b.dma(wt[:], w_v[:, :, col0:col0 + 256], q="pool")
        return wt

    ssq_pending = []

    def flush_ssq():
        while ssq_pending:
            sqb, item = ssq_pending.pop(0)
            for tl in range(4):
                b.mm(c.psum[2][:, item * 4 + tl:item * 4 + tl + 1], sqb[:, tl * 128:(tl + 1) * 128],
                     c.ones1[:, 0:1], start=True, stop=True)

    def sumsq(x_, item):
        sqb = sqbs[item % 2]
        b.act(sqb[:], x_, AF.Square)
        ssq_pending.append((sqb, item))

    def norm_a(item, slot):
        dg = diags[slot % 2]
        pb_ = c.psum[3 + slot % 2]
        b.tt(dg[:], hb(c.ident[:]), rn_all[:, item, :].unsqueeze(2).to_broadcast([128, 4, 128]), ALU.mult)
        b.mm(pb_[:], c.ones32[:], dg[:].rearrange("p h n -> p (h n)"), start=True, stop=True)

    def norm_b(slot, x_, scale, xb_=None):
        pb_ = c.psum[3 + slot % 2]
        b.stt(x_, x_, scale, pb_[:], ALU.mult, ALU.mult)
        if xb_ is not None:
            b.copy(xb_, x_)

    F1, F2, F3, F4 = (c.psum[i] for i in (2, 3, 4, 5))
    S1, S2, S3, S4p = (c.psum[i] for i in (0, 1, 6, 7))
    F3b = F3.bitcast(BF16)

    def v4(p):
        return p[:].rearrange("p (h n) -> p h n", h=4)

    def hsl(h):
        return slice(h * 128, (h + 1) * 128)

    def front(j, jl, ho):
        cols = slice(jl * 128, (jl + 1) * 128)
        g4 = gg[:, j, :]
        beta4 = beta[:, j, :]
        b.tt(Gm[:], hb(c.MUd[:]), bc(g4), ALU.mult)
        b.mm(F1[:], c.ones32[:], flat(Gm), start=True, stop=True)
        b.mm(F2[:, 0:4], c.MUd[:], g4, start=True, stop=True)
        b.mm(F2[:, 4:8], c.SC[:], g4, start=True, stop=True)
        yield
        b.copy(flat(GCrow), F1[:])
        b.copy(sm[:, 0:8], F2[:, 0:8])
        for h in range(4):
            b.tr(F3[:, hsl(h)], KA[:, h, cols], c.ident[:])
        for h in range(4):
            b.tr(F4[:, hsl(h)], VA[:, h, cols], c.ident[:])
        yield
        b.tt(sm[:, 8:12], sm[:, 4:8], sm[:, 0:4], ALU.subtract)
        b.act(sm[:, 8:12], sm[:, 8:12], AF.Exp)
        b.act(sm[:, 12:16], sm[:, 0:4], AF.Exp)
        b.tt(E[:], GCrow[:], bc(sm[:, 0:4]), ALU.subtract)
        yield
        b.ts(E[:], E[:], 0.0, None, ALU.min)
        b.act(EGrow[:], GCrow[:], AF.Exp)
        b.copy(flat(ktok), F3[:])
        b.act(flat(ho["vtok"]), F4[:], AF.Copy)
        yield
        b.act(E[:], E[:], AF.Exp)
        for h in range(4):
            b.mm(F1[:, hsl(h)], KAb[:, h, cols], KAb[:, h, cols], start=True, stop=True)
        for h in range(4):
            b.mm(F2[:, hsl(h)], KAb[:, h, cols], QAb[:, h, cols], start=True, stop=True)
        b.copy(ho["eglast"][:, :, 0:1], EGrow[:, :, 63:64])
        b.copy(ho["eglast"][:, :, 1:2], EGrow[:, :, 127:128])
        yield
        b.tt(EMb[:], E[:], hb(c.MU[:]), ALU.mult)
        b.tt(EMd[:], E[:], hb(c.MUd[:]), ALU.mult)
        b.tt(Rw[:], ktok[:], bc(sm[:, 12:16]), ALU.mult)
        yield
        b.tt(EMb[:], EMb[:], bc(beta4), ALU.mult)
        b.tt(ho["kdec0"][0:64], ktok[0:64], bc(sm[0:64, 8:12], 64), ALU.mult)
        b.tt(ho["kdec1"][64:128], ktok[64:128], bc(sm[64:128, 8:12], 64), ALU.mult)
        yield
        b.tt(Up[:], v4(F1), EMb[:], ALU.mult)
        b.tt(ho["qdec"][:], QAb[:, :, cols], EGrow[:], ALU.mult)
        yield
        b.tt(ho["attnT"][:], v4(F2), EMd[:], ALU.mult)
        for h in range(4):
            b.tr(F3b[:, hsl(h)], Up[:, h, :], c.identb[:])
        b.tt(X[:], hb(c.ident[:]), Up[:], ALU.subtract)
        yield
        b.copy(flat(UpT), F3b[:, 0:512])
        yield
        cur, nxt = (Up, UpT), (P, PT)
        Xc, Xn = X, X2
        for lev in range(5):
            if lev == 4:
                Xn = ho["Xf"]
            if lev < 4:
                for h in range(4):
                    b.mm(F1[:, hsl(h)], cur[1][:, h, :], cur[0][:, h, :], start=True, stop=True)
            for h in range(4):
                b.mm(F2[:, hsl(h)], cur[0][:, h, :], cur[1][:, h, :], start=True, stop=True)
            yield
            b.act(flat(nxt[1]), F2[:], AF.Copy)
            if lev < 4:
                b.copy(flat(nxt[0]), F1[:])
            yield
            for h in range(4):
                b.mm(F4[:, hsl(h)], nxt[1][:, h, :], Xc[:, h, :], start=True, stop=True)
            yield
            b.tt(Xn[:], v4(F4), Xc[:], ALU.add)
            yield
            cur, nxt = nxt, cur
            Xc, Xn = Xn, Xc
        for h in range(4):
            b.mm(F3[:, hsl(h)], Rw[:, h, :], ho["Xf"][:, h, :], start=True, stop=True)
        yield
        b.act(flat(ho["nw0T"]), F3[:], AF.Copy, scale=-1.0)
        yield

    def state(j, jl, ho):
        cols = slice(jl * 128, (jl + 1) * 128)
        gcols = slice(j * 128, (j + 1) * 128)
        kdec = (ho["kdec0"], ho["kdec1"])
        for ch in range(2):
            cs = slice(ch * 64, ch * 64 + 64)
            pV, pO, pS = (S1, S2, S3) if ch == 0 else (S4p, S2, S3)
            for h in range(4):
                b.mm(pV[:, hsl(h)], ho["Xf"][:, h, :], ho["vtok"][:, h, :], start=True, stop=False)
                b.mm(pV[:, hsl(h)], ho["nw0T"][:, h, :], S4b[:, h, :], start=False, stop=True)
            yield
            b.tt(vnew[cs], v4(pV)[cs], bc(beta[cs, j, :], 64), ALU.mult)
            yield
            for h in range(4):
                b.mm(pS[:, hsl(h)], kdec[ch][:, h, :], vnew[:, h, :], start=True, stop=True)
            for h in range(4):
                b.mm(pO[:, hsl(h)], ho["qdec"][:, h, :], S4b[:, h, :], start=True, stop=False)
                b.mm(pO[:, hsl(h)], ho["attnT"][:, h, :], vnew[:, h, :], start=False, stop=True)
            b.tt(S4[:], S4[:], ho["eglast"][:, :, ch:ch + 1].to_broadcast([128, 4, 128]), ALU.mult)
            yield
            b.tt(S4[:], S4[:], v4(pS), ALU.add)
            b.act(otok[cs], v4(pO)[cs], AF.Copy)
            yield
            b.act(S4b[:], S4[:], AF.Copy)
            yield
        b.tt(junk[:], otok[:], otok[:], ALU.mult)
        yield
        c.s.add("dve", lambda e, o_=smS[:, 0:4], i_=junk[:]: e.reduce_sum(o_, i_, mybir.AxisListType.X),
                reads=[junk[:]], writes=[smS[:, 0:4]])
        yield
        b.ts(smS[:, 4:8], smS[:, 0:4], 1.0 / 128.0, NORM_EPS, ALU.mult, ALU.add)
        yield
        b.recip(smS[:, 4:8], smS[:, 4:8])
        yield
        b.act(smS[:, 4:8], smS[:, 4:8], AF.Sqrt)
        yield
        b.tt(otok[:], otok[:], bc(smS[:, 4:8]), ALU.mult)
        yield
        for h in range(4):
            b.tr(S1[:, hsl(h)], otok[:, h, :], c.ident[:])
        yield
        b.stt(o_a[:, :, gcols], v4(S1), c.ng[:, 0:1], SZ[:, :, cols], ALU.mult, ALU.mult)
        yield

    def drain(g):
        for _ in g:
            pass

    def merge(g1, g2):
        a_alive = b_alive = True
        while a_alive or b_alive:
            if a_alive:
                try:
                    next(g1)
                except StopIteration:
                    a_alive = False
            if b_alive:
                try:
                    next(g2)
                except StopIteration:
                    b_alive = False

    for qi in range(4):
        items = []
        for pair in range(2):
            for kind, base in (("q", 0), ("k", 512), ("v", 1024), ("z", 1544)):
                for hh in range(2):
                    h = pair * 2 + hh
                    it_ = dict(kind=kind, nm=f"{kind}a{h}", h=h, hh=hh, load=(base + pair * 256) if hh == 0 else None,
                               ss=None)
                    if kind == "q":
                        it_.update(acc=qtmps[hh][:], out=QAb[:, h, :], ss=h)
                    elif kind == "k":
                        it_.update(acc=KA[:, h, :], out=KA[:, h, :], ss=4 + h)
                    elif kind == "v":
                        it_.update(acc=VA[:, h, :], out=VA[:, h, :])
                    items.append(it_)
        cur_w = [None]

        def part_a(it_):
            if it_["load"] is not None:
                cur_w[0] = load_pair(it_["load"])
            it_["wt"] = cur_w[0]
            conv_a(it_, qi)

        part_a(items[0])
        for i_, it_ in enumerate(items):
            if i_ + 1 < len(items):
                part_a(items[i_ + 1])
            conv_b(it_, qi)
        flush_ssq()
        rflat = rn_all[:].rearrange("p i t -> p (i t)")
        b.ts(rflat, c.psum[2][:, 0:32], NORM_EPS, None, ALU.add)
        b.recip(rflat, rflat)
        b.act(rflat, rflat, AF.Sqrt)
        nitems = []
        for h in range(4):
            nitems.append((h, QAb[:, h, :], 128.0 ** -0.5, None))
            nitems.append((4 + h, KA[:, h, :], 1.0, KAb[:, h, :]))
        norm_a(nitems[0][0], 0)
        for i_, (item, x_, sc_, xb_) in enumerate(nitems):
            if i_ + 1 < len(nitems):
                norm_a(nitems[i_ + 1][0], i_ + 1)
            norm_b(i_, x_, sc_, xb_)
        drain(front(qi * 4, 0, HO[0]))
        for jl in range(4):
            j = qi * 4 + jl
            if jl + 1 < 4:
                merge(front(j + 1, jl + 1, HO[(jl + 1) % 2]), state(j, jl, HO[jl % 2]))
            else:
                drain(state(j, jl, HO[jl % 2]))


def phase_G(c, l, o_a, o_b, o_c, A, lnidx, defer_last=False):
    s, b, W = c.s, c.b, c.W
    NTOK = 1024
    merged = A("merged", [128, KD, NTOK], BF16)
    wg = [[A(f"wGg{i}{br}", [128, KD, 128], BF16) for br in range(3)] for i in range(2)]
    wbr = [[A(f"wGb{i}{br}", [128, 4, 128], BF16) for br in range(3)] for i in range(2)]
    wo = [A(f"wGo{i}", [128, KD, 128], BF16) for i in range(2)]
    sg = [A(f"sgG{i}", [128, 512], F32) for i in range(2)]
    accb = A("accG", [128, 512], F32)
    ln_off = A.o
    assert ln_off + 2 * KD * 256 * 2 + 2048 <= ARENA_END, ln_off
    wg_v = W["w_gate"][l].rearrange("(k p) c -> p k c", p=128)
    wbr_v = [W[nm][l].rearrange("(f p) c -> p f c", p=128) for nm in ("w_br_a", "w_br_b", "w_br_c")]
    wo_v = W["w_out"][l].rearrange("(k p) c -> p k c", p=128)
    obr = (o_a, o_b, o_c)
    wc = 0
    pc = 0
    pendingG = Pending()
    for st in range(T // NTOK):
        c0 = st * NTOK
        for dc in range(KD):
            i = wc % 2
            wc += 1
            for br in range(3):
                b.dma(wg[i][br][:], wg_v[:, :, br * D + dc * 128:br * D + (dc + 1) * 128], q="pool")
                b.dma(wbr[i][br][:], wbr_v[br][:, :, dc * 128:(dc + 1) * 128], q="pool")
            for nt in range(NTOK // 512):
                cols = slice(c0 + nt * 512, c0 + (nt + 1) * 512)
                for br in range(3):
                    pg = c.psum[pc % 2]
                    pb_ = c.psum[2 + pc % 2]
                    sgb = sg[pc % 2]
                    pc += 1
                    for k in range(KD):
                        b.mm(pg[:], wg[i][br][:, k, :], c.XB[:, k, cols], start=(k == 0), stop=(k == KD - 1))
                    for f in range(4):
                        b.mm(pb_[:], wbr[i][br][:, f, :], obr[br][:, f, cols], start=(f == 0), stop=(f == 3))
                    b.act(sgb[:], pg[:], AF.Sigmoid, bias=c.bgate[:, br * 8 + dc:br * 8 + dc + 1])
                    if br == 0:
                        b.tt(accb[:], sgb[:], pb_[:], ALU.mult)
                    elif br == 1:
                        b.tt(sgb[:], sgb[:], pb_[:], ALU.mult)
                        b.tt(accb[:], accb[:], sgb[:], ALU.add)
                    else:
                        b.tt(sgb[:], sgb[:], pb_[:], ALU.mult)
                        b.tt(merged[:, dc, nt * 512:(nt + 1) * 512], accb[:], sgb[:], ALU.add)
                    pendingG.step(3)
        for dc in range(KD):
            wob = wo[dc % 2]
            b.dma(wob[:], wo_v[:, :, dc * 128:(dc + 1) * 128], q="pool")
            for nt in range(NTOK // 512):
                cols = slice(c0 + nt * 512, c0 + (nt + 1) * 512)
                py = c.psum[4 + pc % 2]
                pc += 1
                for k in range(KD):
                    b.mm(py[:], wob[:, k, :], merged[:, k, nt * 512:(nt + 1) * 512], start=(k == 0), stop=(k == KD - 1))
                xs = c.X32[:, dc, cols]
                b.stt(xs, xs, ALPHA, py[:], ALU.mult, ALU.add)
        pendingG.drain()
        for q in range(NTOK // 256):
            cols = slice(c0 + q * 256, c0 + (q + 1) * 256)
            pendingG.add(layer_norm_gen(c, cols, lnidx, ln_off))
    if defer_last:
        c.deferred = pendingG
    else:
        pendingG.drain()


def mixer_phase(c, l, defer_last=False):
    import os
    load_mixer_layer_consts(c, l)
    A0 = Alloc(c, OFF_ARENA)
    o_b = A0("o_b", [128, 4, T], BF16)
    baseB = A0.o
    o_a = A0("o_a", [128, 4, T], BF16)
    baseA = A0.o
    o_c = A0("o_c", [128, 4, T], BF16)
    baseC = A0.o
    ph = os.environ.get("MIX_PH", "BACG")
    if "B" in ph:
        phase_B(c, l, o_b, Alloc(c, baseB))
    if "A" in ph:
        phase_A(c, l, o_a, Alloc(c, baseA))
    if "C" in ph:
        phase_C(c, l, o_c, Alloc(c, baseC))
    if "G" in ph:
        phase_G(c, l, o_a, o_b, o_c, Alloc(c, baseC), l * 3 + 1, defer_last=defer_last)
```
